# Optimizing a Trainium2 kernel written in Bass

```python
import math
import jax, jax.numpy as jnp
from jax import lax
import numpy as np

D_MODEL = 1024
BATCH = 4
SEQ = 8192
DEPTH = 2

N_A_LAYERS = DEPTH // 2
N_B_LAYERS = DEPTH - N_A_LAYERS
N_HEADS = 16
HEAD_DIM = 64
N_KV_GROUPS = 4
HEADS_PER_GROUP = N_HEADS // N_KV_GROUPS
D_FF = 4 * D_MODEL
CONV_WIDTH = 3
CMP_BLOCK = 32
CMP_STRIDE = 16
SEL_BLOCK = 64
SEL_TOP_N = 16
WINDOW = 512
PHI_HIDDEN = 256
N_BUCKETS = 32
MAX_DISTANCE = 128
Q_BLOCK = 128
EPS = 1e-6
NEG_INF = -1e30
FORCED_SCORE = 1e9

kernel_name = "yoco_shortconv_nsa_hybrid"


def rms_norm(x, g):
    xf = x.astype(jnp.float32)
    y = xf * lax.rsqrt(jnp.mean(xf * xf, axis=-1, keepdims=True) + EPS)
    return (y * g.astype(jnp.float32)).astype(x.dtype)


def t5_bucket(dist):
    dist = jnp.maximum(dist, 0)
    max_exact = N_BUCKETS // 2
    logd = jnp.log(jnp.maximum(dist, 1).astype(jnp.float32) / max_exact)
    large = max_exact + (logd / math.log(MAX_DISTANCE / max_exact)
                         * (N_BUCKETS - max_exact)).astype(jnp.int32)
    large = jnp.minimum(large, N_BUCKETS - 1)
    return jnp.where(dist < max_exact, dist, large)


def short_conv_mixer(h, w_in, conv_w, w_out):
    S = h.shape[1]
    b_gate, c_gate, u = jnp.split(h @ w_in, 3, axis=-1)
    v = c_gate * u
    vp = jnp.pad(v, ((0, 0), (CONV_WIDTH - 1, 0), (0, 0)))
    conv = sum(conv_w[k] * vp[:, k:k + S] for k in range(CONV_WIDTH))
    return (b_gate * conv) @ w_out


def squared_relu_mlp(h, w1, w2):
    return jnp.square(jax.nn.relu(h @ w1)) @ w2


def compress_blocks(t, pe, w1, w2):
    B, S, G, dh = t.shape
    chunks = t.reshape(B, S // CMP_STRIDE, CMP_STRIDE, G, dh)
    blocks = jnp.concatenate([chunks[:, :-1], chunks[:, 1:]], axis=2)
    blocks = blocks + pe[None, None, :, None, :]
    n_cmp = blocks.shape[1]
    flat = jnp.moveaxis(blocks, 2, 3).reshape(B, n_cmp, G, CMP_BLOCK * dh)
    return jax.nn.silu(flat @ w1) @ w2


def nsa_shared_kv(h, w_kv, cmp_pe_k, cmp_pe_v, phi_k_w1, phi_k_w2, phi_v_w1, phi_v_w2):
    B, S, _ = h.shape
    kv = (h @ w_kv).reshape(B, S, 6, N_KV_GROUPS, HEAD_DIM)
    k_raw, v_raw = kv[:, :, 0], kv[:, :, 1]
    k_sel, v_sel = kv[:, :, 2], kv[:, :, 3]
    k_win, v_win = kv[:, :, 4], kv[:, :, 5]
    k_cmp = compress_blocks(k_raw, cmp_pe_k, phi_k_w1, phi_k_w2)
    v_cmp = compress_blocks(v_raw, cmp_pe_v, phi_v_w1, phi_v_w2)
    return (k_cmp, v_cmp, k_sel, v_sel, k_win, v_win)


def nsa_attention(q, gates, k_cmp, v_cmp, k_sel, v_sel, k_win, v_win, rel_bias):
    B, S, G, Hg, dh = q.shape
    n_cmp = k_cmp.shape[1]
    n_selb = S // SEL_BLOCK
    top_n = min(SEL_TOP_N, n_selb)
    n_qb = S // Q_BLOCK
    bias_g = rel_bias.reshape(N_BUCKETS, G, Hg)

    cmp_end = jnp.arange(n_cmp) * CMP_STRIDE + CMP_BLOCK - 1
    ratio = SEL_BLOCK // CMP_STRIDE
    span = CMP_BLOCK // CMP_STRIDE
    offs = (jnp.arange(ratio)[:, None] - jnp.arange(span)[None, :]).reshape(-1)
    map_idx = ratio * jnp.arange(n_selb)[:, None] + offs[None, :]
    map_ok = (map_idx >= 0) & (map_idx < n_cmp)
    map_idx = jnp.clip(map_idx, 0, n_cmp - 1)

    k_sel_bg = jnp.transpose(k_sel.reshape(B, n_selb, SEL_BLOCK, G, dh), (0, 3, 1, 2, 4))
    v_sel_bg = jnp.transpose(v_sel.reshape(B, n_selb, SEL_BLOCK, G, dh), (0, 3, 1, 2, 4))
    k_win_p = jnp.pad(k_win, ((0, 0), (WINDOW, 0), (0, 0), (0, 0)))
    v_win_p = jnp.pad(v_win, ((0, 0), (WINDOW, 0), (0, 0), (0, 0)))
    b_ix = jnp.arange(B)[:, None, None, None]
    g_ix = jnp.arange(G)[None, None, :, None]
    jb = jnp.arange(n_selb)

    def one_block(qb):
        t0 = qb * Q_BLOCK
        qt = lax.dynamic_slice_in_dim(q, t0, Q_BLOCK, axis=1)
        gt = lax.dynamic_slice_in_dim(gates, t0, Q_BLOCK, axis=1)
        pos = t0 + jnp.arange(Q_BLOCK)

        d_c = pos[:, None] - cmp_end[None, :]
        valid_c = d_c >= 0
        bias_c = jnp.transpose(bias_g[t5_bucket(d_c)], (0, 2, 3, 1)).astype(jnp.float32)
        logit_c = jnp.einsum('bqghd,bngd->bqghn', qt, k_cmp).astype(jnp.float32) + bias_c[None]
        logit_c = jnp.where(valid_c[None, :, None, None, :], logit_c, NEG_INF)
        has_any = jnp.any(valid_c, axis=-1).astype(jnp.float32)
        p_c = jax.nn.softmax(logit_c, axis=-1) * has_any[None, :, None, None, None]
        o_c = jnp.einsum('bqghn,bngd->bqghd', p_c.astype(v_cmp.dtype), v_cmp)

        imp = p_c.sum(axis=3)
        imp_s = jnp.sum(jnp.where(map_ok, jnp.take(imp, map_idx, axis=-1), 0.0), axis=-1)
        cb = pos // SEL_BLOCK
        forced = (jb[None, :] == 0) | (jb[None, :] == cb[:, None]) | (jb[None, :] == cb[:, None] - 1)
        future = jb[None, :] > cb[:, None]
        score = jnp.where(forced[None, :, None, :], FORCED_SCORE, imp_s)
        score = jnp.where(future[None, :, None, :], NEG_INF, score)
        top_s, top_idx = lax.top_k(score, top_n)
        blk_ok = top_s > NEG_INF * 0.5

        ks = k_sel_bg[b_ix, g_ix, top_idx].reshape(B, Q_BLOCK, G, top_n * SEL_BLOCK, dh)
        vs = v_sel_bg[b_ix, g_ix, top_idx].reshape(B, Q_BLOCK, G, top_n * SEL_BLOCK, dh)
        kpos = (top_idx[..., None] * SEL_BLOCK + jnp.arange(SEL_BLOCK)).reshape(B, Q_BLOCK, G, -1)
        blk_ok_k = jnp.repeat(blk_ok, SEL_BLOCK, axis=-1)
        d_s = pos[None, :, None, None] - kpos
        valid_s = blk_ok_k & (d_s >= 0)
        bias_s = jnp.transpose(bias_g[t5_bucket(d_s), g_ix], (0, 1, 2, 4, 3)).astype(jnp.float32)
        logit_s = jnp.einsum('bqghd,bqgkd->bqghk', qt, ks).astype(jnp.float32) + bias_s
        logit_s = jnp.where(valid_s[:, :, :, None, :], logit_s, NEG_INF)
        p_s = jax.nn.softmax(logit_s, axis=-1)
        o_s = jnp.einsum('bqghk,bqgkd->bqghd', p_s.astype(vs.dtype), vs)

        kw = lax.dynamic_slice_in_dim(k_win_p, t0, Q_BLOCK + WINDOW, axis=1)
        vw = lax.dynamic_slice_in_dim(v_win_p, t0, Q_BLOCK + WINDOW, axis=1)
        wpos = t0 - WINDOW + jnp.arange(Q_BLOCK + WINDOW)
        d_w = pos[:, None] - wpos[None, :]
        valid_w = (d_w >= 0) & (d_w < WINDOW) & (wpos[None, :] >= 0)
        bias_w = jnp.transpose(bias_g[t5_bucket(d_w)], (0, 2, 3, 1)).astype(jnp.float32)
        logit_w = jnp.einsum('bqghd,bkgd->bqghk', qt, kw).astype(jnp.float32) + bias_w[None]
        logit_w = jnp.where(valid_w[None, :, None, None, :], logit_w, NEG_INF)
        p_w = jax.nn.softmax(logit_w, axis=-1)
        o_w = jnp.einsum('bqghk,bkgd->bqghd', p_w.astype(vw.dtype), vw)

        o = gt[..., 0:1] * o_c + gt[..., 1:2] * o_s + gt[..., 2:3] * o_w
        return o.astype(q.dtype)

    o = lax.map(one_block, jnp.arange(n_qb))
    return jnp.moveaxis(o, 0, 1).reshape(B, S, G, Hg, dh)


def nsa_mixer(h, w_qg, w_o, shared, rel_bias):
    B, S, _ = h.shape
    qg = h @ w_qg
    q = qg[..., :N_HEADS * HEAD_DIM].reshape(B, S, N_KV_GROUPS, HEADS_PER_GROUP, HEAD_DIM)
    q = q * (HEAD_DIM ** -0.5)
    gates = jax.nn.sigmoid(qg[..., N_HEADS * HEAD_DIM:].astype(jnp.float32))
    gates = gates.reshape(B, S, N_KV_GROUPS, HEADS_PER_GROUP, 3)
    k_cmp, v_cmp, k_sel, v_sel, k_win, v_win = shared
    o = nsa_attention(q, gates, k_cmp, v_cmp, k_sel, v_sel, k_win, v_win, rel_bias)
    return o.reshape(B, S, N_HEADS * HEAD_DIM) @ w_o


def setup_inputs(seed: int = 0) -> dict:
    key = jax.random.key(seed)
    ks = jax.random.split(key, 22)
    f32 = jnp.float32
    D, H, G, dh = D_MODEL, N_HEADS, N_KV_GROUPS, HEAD_DIM

    def nrm(k, shape, scale):
        return jax.random.normal(k, shape, f32) * scale

    return {
        "x": nrm(ks[0], (BATCH, SEQ, D), 1.0),
        "rel_bias": nrm(ks[1], (N_BUCKETS, H), 0.5),
        "norm_mix": 1.0 + nrm(ks[2], (DEPTH, D), 0.1),
        "norm_mlp": 1.0 + nrm(ks[3], (DEPTH, D), 0.1),
        "mlp_w1": nrm(ks[4], (DEPTH, D, D_FF), D ** -0.5),
        "mlp_w2": nrm(ks[5], (DEPTH, D_FF, D), 0.5 * D_FF ** -0.5),
        "a_w_in": nrm(ks[6], (N_A_LAYERS, D, 3 * D), D ** -0.5),
        "a_conv_w": nrm(ks[7], (N_A_LAYERS, CONV_WIDTH, D), CONV_WIDTH ** -0.5),
        "a_w_out": nrm(ks[8], (N_A_LAYERS, D, D), D ** -0.5),
        "kv_norm": 1.0 + nrm(ks[9], (D,), 0.1),
        "w_kv": nrm(ks[10], (D, 6 * G * dh), D ** -0.5),
        "cmp_pe_k": nrm(ks[11], (CMP_BLOCK, dh), 0.1),
        "cmp_pe_v": nrm(ks[12], (CMP_BLOCK, dh), 0.1),
        "phi_k_w1": nrm(ks[13], (CMP_BLOCK * dh, PHI_HIDDEN), (CMP_BLOCK * dh) ** -0.5),
        "phi_k_w2": nrm(ks[14], (PHI_HIDDEN, dh), PHI_HIDDEN ** -0.5),
        "phi_v_w1": nrm(ks[15], (CMP_BLOCK * dh, PHI_HIDDEN), (CMP_BLOCK * dh) ** -0.5),
        "phi_v_w2": nrm(ks[16], (PHI_HIDDEN, dh), PHI_HIDDEN ** -0.5),
        "b_w_qg": nrm(ks[17], (N_B_LAYERS, D, H * dh + 3 * H), D ** -0.5),
        "b_w_o": nrm(ks[18], (N_B_LAYERS, H * dh, D), (H * dh) ** -0.5),
        "final_norm": 1.0 + nrm(ks[19], (D,), 0.1),
    }


def reference(x, rel_bias, norm_mix, norm_mlp, mlp_w1, mlp_w2, a_w_in, a_conv_w, a_w_out,
              kv_norm, w_kv, cmp_pe_k, cmp_pe_v, phi_k_w1, phi_k_w2, phi_v_w1, phi_v_w2,
              b_w_qg, b_w_o, final_norm):
    shared = None
    for layer in range(DEPTH):
        h = rms_norm(x, norm_mix[layer])
        if layer < N_A_LAYERS:
            x = x + short_conv_mixer(h, a_w_in[layer], a_conv_w[layer], a_w_out[layer])
        else:
            i = layer - N_A_LAYERS
            x = x + nsa_mixer(h, b_w_qg[i], b_w_o[i], shared, rel_bias)
        x = x + squared_relu_mlp(rms_norm(x, norm_mlp[layer]), mlp_w1[layer], mlp_w2[layer])
        if layer == N_A_LAYERS - 1:
            shared = nsa_shared_kv(rms_norm(x, kv_norm), w_kv, cmp_pe_k, cmp_pe_v,
                                   phi_k_w1, phi_k_w2, phi_v_w1, phi_v_w2)
    return rms_norm(x, final_norm)
```

```python
import os
import math
import contextlib
import numpy as np
import ml_dtypes
import concourse.bass as bass
import concourse.mybir as mybir


F32 = mybir.dt.float32
BF16 = mybir.dt.bfloat16
AF = mybir.ActivationFunctionType
ALU = mybir.AluOpType

SEM_CAP = 30000
DMA_RING = 8


class Buf:
    __slots__ = ("name", "w", "r", "multi", "ws")

    def __init__(self, name="", multi=False):
        self.name = name
        self.w = None
        self.r = []
        self.multi = multi
        self.ws = []


class Op:
    __slots__ = ("eng", "fn", "deps", "dma", "sig", "sem", "val", "prev_ring")

    def __init__(self, eng, fn, deps, dma):
        self.eng = eng
        self.fn = fn
        self.deps = deps
        self.dma = dma
        self.sig = False
        self.sem = None
        self.val = 0
        self.prev_ring = None


class Prog:
    ENGS = ("pe", "act", "dve", "pool", "sp")

    def __init__(self, nc):
        self.nc = nc
        self.ops = []
        self.stack = contextlib.ExitStack()
        self.nbuf = 0

    def init_arena(self, nbytes):
        self.arena = self.stack.enter_context(self.nc.sbuf_tensor("sb_arena", [128, nbytes // 2], BF16))
        self.arena_bytes = nbytes
        self.aoff = 0

    def sbuf(self, name, shape, dtype):
        shape = list(shape)
        n = 1
        for d in shape[1:]:
            n *= d
        esz = 4 if dtype == F32 else 2
        nb = (n * esz + 31) // 32 * 32
        if self.aoff + nb > self.arena_bytes:
            raise RuntimeError(f"arena overflow allocating {name} {shape}: need {nb}, used {self.aoff}/{self.arena_bytes}")
        v = self.arena[:, self.aoff // 2:self.aoff // 2 + n * esz // 2]
        self.aoff += nb
        if esz == 4:
            v = v.bitcast(F32)
        if len(shape) > 2:
            names = [f"d{i}" for i in range(len(shape) - 1)]
            pat = "p (" + " ".join(names) + ") -> p " + " ".join(names)
            v = v.rearrange(pat, **{nm: d for nm, d in zip(names, shape[1:])})
        if shape[0] < 128:
            v = v[0:shape[0]]
        return v

    def barrier(self):
        last = {}
        dmas = []
        for i, op in enumerate(self.ops):
            if op.dma:
                dmas.append(i)
            elif op.fn is not None:
                last[op.eng] = i
        start = getattr(self, "_bar_from", 0)
        dmas = [i for i in dmas if i >= start]
        deps = set(last.values()) | set(dmas)
        for e in self.ENGS:
            self.ops.append(Op(e, None, set(deps), False))
        self._bar_from = len(self.ops)

    def psum(self, name, shape, dtype=F32):
        t = self.stack.enter_context(self.nc.psum_tensor("ps_" + name, list(shape), dtype))
        return t

    def buf(self, name="", multi=False):
        self.nbuf += 1
        return Buf(name or f"b{self.nbuf}", multi)

    def add(self, eng, fn, rd=(), wr=(), dma=False):
        deps = set()
        for b in rd:
            if b.multi:
                deps.update(b.ws)
            elif b.w is not None:
                deps.add(b.w)
        for b in wr:
            if (not b.multi) and b.w is not None:
                deps.add(b.w)
            deps.update(b.r)
        idx = len(self.ops)
        self.ops.append(Op(eng, fn, deps, dma))
        for b in rd:
            b.r.append(idx)
        for b in wr:
            if b.multi:
                b.ws.append(idx)
            else:
                b.w = idx
            b.r = []
        return idx

    def dma(self, q, out, in_, rd=(), wr=(), **kw):
        return self.add(q, lambda e: e.dma_start(out=out, in_=in_, **kw), rd, wr, dma=True)

    def mm(self, out, lhsT, rhs, start, stop, rd=(), wr=(), **kw):
        return self.add("pe", lambda e: e.matmul(out, lhsT, rhs, start=start, stop=stop, **kw), rd, wr)

    def emit(self):
        nc = self.nc
        ops = self.ops
        for i, op in enumerate(ops):
            keep = set()
            best = {}
            for d in op.deps:
                o = ops[d]
                if o.dma:
                    keep.add(d)
                else:
                    if o.eng == op.eng and op.eng == "pe" and not op.dma:
                        continue
                    if o.eng not in best or best[o.eng] < d:
                        best[o.eng] = d
            keep.update(best.values())
            op.deps = keep
            for d in keep:
                ops[d].sig = True
        ring_cnt = {}
        ring_last = {}
        for i, op in enumerate(ops):
            if op.dma:
                n = ring_cnt.get(op.eng, 0)
                ring_cnt[op.eng] = n + 1
                slot = n % DMA_RING
                key = (op.eng, slot)
                op.prev_ring = ring_last.get(key)
                ring_last[key] = i
                op.sem = ("dma", op.eng, slot)
                op.val = 16 * (n // DMA_RING + 1)
                op.sig = True
        cnt = {e: 0 for e in self.ENGS}
        for op in ops:
            if op.sig and not op.dma:
                n = cnt[op.eng]
                cnt[op.eng] = n + 1
                op.sem = ("eng", op.eng, n // SEM_CAP)
                op.val = n % SEM_CAP + 1
        semnames = []
        for op in ops:
            if op.sig and op.sem not in semnames:
                semnames.append(op.sem)
        sems = {}
        for sn in semnames:
            sems[sn] = self.stack.enter_context(nc.semaphore("s_" + "_".join(str(x) for x in sn)))
        self.nsems = len(sems)

        per_eng = {e: [] for e in self.ENGS}
        for i, op in enumerate(ops):
            per_eng[op.eng].append(i)

        def run_engine(ename, eh):
            waited = {}
            for i in per_eng[ename]:
                op = ops[i]
                need = []
                for d in op.deps:
                    o = ops[d]
                    need.append((o.sem, o.val))
                if op.prev_ring is not None:
                    o = ops[op.prev_ring]
                    need.append((o.sem, o.val))
                for sn, v in need:
                    if waited.get(sn, 0) >= v:
                        continue
                    eh.wait_ge(sems[sn], v)
                    waited[sn] = v
                ins = op.fn(eh) if op.fn is not None else None
                if op.sig:
                    if ins is None:
                        raise RuntimeError("signaling op returned None")
                    ins.then_inc(sems[op.sem], 16 if op.dma else 1)

        with nc.Block() as block:
            @block.sync
            def _(e):
                run_engine("sp", e)

            @block.scalar
            def _(e):
                run_engine("act", e)

            @block.vector
            def _(e):
                run_engine("dve", e)

            @block.gpsimd
            def _(e):
                run_engine("pool", e)

            @block.tensor
            def _(e):
                run_engine("pe", e)

    def close(self):
        self.stack.close()


EPS = 1e-6
S = 8192
NT = S // 512
NSLAB_A = 24 + 8 + 32 + 32 + 8 + 4


def host_slabs_A(inp):
    def slab(W, cols):
        return np.ascontiguousarray(W[:, cols].reshape(8, 128, 128).transpose(1, 0, 2))
    out = []
    w_in = inp["a_w_in"][0]
    for i in range(8):
        for part in range(3):
            out.append(slab(w_in, part * 1024 + i * 128 + np.arange(128)))
    w_out = inp["a_w_out"][0]
    for i in range(8):
        out.append(slab(w_out, i * 128 + np.arange(128)))
    w1 = inp["mlp_w1"][0]
    for j in range(32):
        out.append(slab(w1, j * 128 + np.arange(128)))
    w2 = inp["mlp_w2"][0]
    for i in range(8):
        for kq in range(4):
            out.append(slab(w2[kq * 1024:(kq + 1) * 1024], i * 128 + np.arange(128)))
    wkv = inp["w_kv"]
    def kvcols(ty, g):
        return ty * 256 + g * 64 + np.arange(64)
    for ty in (2, 4, 0, 1):
        for P in range(2):
            out.append(slab(wkv, np.concatenate([kvcols(ty, 2 * P), kvcols(ty, 2 * P + 1)])))
    vcols = np.concatenate([3 * 256 + np.arange(256), 5 * 256 + np.arange(256)])
    wv = wkv[:, vcols].reshape(8, 128, 512)
    for j in range(4):
        out.append(np.ascontiguousarray(wv[2 * j:2 * j + 2].transpose(1, 0, 2)).reshape(128, 8, 128))
    return np.stack(out).reshape(len(out), 128, 1024).astype(np.float32)


def host_small_A(inp, parity):
    def cvec(v):
        return v.reshape(8, 128).T
    g = np.stack([cvec(inp["norm_mix"][0]), cvec(inp["norm_mlp"][0]), cvec(inp["kv_norm"])], axis=1)
    cw = inp["a_conv_w"][0]
    cwl = np.stack([cvec(cw[k]) for k in range(3)], axis=2)
    par = np.zeros((128, 2), np.float32)
    par[:, 0] = 1.0 - parity
    par[:, 1] = parity
    def phi1(w):
        a = w.reshape(32, 64, 256).transpose(1, 0, 2)
        return np.concatenate([a, a], axis=0).reshape(128, 32 * 256)
    ph1 = np.stack([phi1(inp["phi_k_w1"]), phi1(inp["phi_v_w1"])])
    w2k = inp["phi_k_w2"].reshape(2, 128, 64).transpose(1, 0, 2)
    ph2k = np.concatenate([w2k, w2k], axis=2)
    ph2v = inp["phi_v_w2"].reshape(2, 128, 64).transpose(1, 0, 2)
    def peT(pe):
        return np.concatenate([pe.T, pe.T], axis=0)
    pe = np.stack([peT(inp["cmp_pe_k"]), peT(inp["cmp_pe_v"])])
    f = lambda a: np.ascontiguousarray(a, dtype=np.float32)
    return dict(gA=f(g.reshape(128, 24)), cwA=f(cwl.reshape(128, 24)), par=f(par),
                ph1=f(ph1), ph2k=f(ph2k.reshape(128, 256)), ph2v=f(ph2v.reshape(128, 128)), peT=f(pe))


class Banks:
    def __init__(self, prog, n=8):
        self.t = [prog.psum(f"bank{i}", [128, 512], F32) for i in range(n)]
        self.b = [prog.buf(f"bank{i}") for i in range(n)]
        self.i = 0
        self.n = n

    def next(self):
        k = self.i % self.n
        self.i += 1
        return self.t[k], self.b[k]


def cast_weights(prog, src, dst, nslab, dstbuf, tag):
    nst = 3
    key = ("cast_stage", prog.aoff_epoch if hasattr(prog, "aoff_epoch") else 0)
    if getattr(prog, "_cast_stage_key", None) != id(prog.ops) or getattr(prog, "_cast_stage_mark", None) is None or prog._cast_stage_mark > prog.aoff:
        st32 = [prog.sbuf(f"{tag}c32_{i}", [128, 1024], F32) for i in range(nst)]
        st16 = [prog.sbuf(f"{tag}c16_{i}", [128, 1024], BF16) for i in range(nst)]
        b32 = [prog.buf() for _ in range(nst)]
        b16 = [prog.buf() for _ in range(nst)]
        prog._cast_stage = (st32, st16, b32, b16)
        prog._cast_stage_key = id(prog.ops)
        prog._cast_stage_mark = prog.aoff
    st32, st16, b32, b16 = prog._cast_stage
    slabs = list(nslab) if not isinstance(nslab, int) else list(range(nslab))
    cnt = getattr(prog, "_cast_cnt", 0)

    def load(j):
        s = slabs[j]
        k = (cnt + j) % nst
        prog.dma("pool", st32[k][:, :], src[s], wr=[b32[k]])

    def cast_store(j):
        s = slabs[j]
        k = (cnt + j) % nst
        r = (cnt + j) % 3
        if r == 0:
            prog.add("dve", lambda e, k=k: e.tensor_copy(out=st16[k][:, :], in_=st32[k][:, :]), rd=[b32[k]], wr=[b16[k]])
        elif r == 1:
            prog.add("act", lambda e, k=k: e.copy(out=st16[k][:, :], in_=st32[k][:, :]), rd=[b32[k]], wr=[b16[k]])
        else:
            prog.add("pool", lambda e, k=k: e.tensor_copy(out=st16[k][:, :], in_=st32[k][:, :]), rd=[b32[k]], wr=[b16[k]])
        prog.dma("pool", dst[s], st16[k][:, :], rd=[b16[k]], wr=[dstbuf[s]])

    LA = 2
    n = len(slabs)
    for j in range(n + LA):
        if j < n:
            load(j)
        if j - LA >= 0:
            cast_store(j - LA)
    prog._cast_cnt = cnt + n


class WStream:
    def __init__(self, prog, name, nslots, src, srcbuf):
        self.prog = prog
        self.slots = [prog.sbuf(f"{name}_{i}", [128, 1024], BF16) for i in range(nslots)]
        self.bufs = [prog.buf(f"{name}_{i}") for i in range(nslots)]
        self.n = 0
        self.src = src
        self.srcbuf = srcbuf

    def fetch(self, s):
        k = self.n % len(self.slots)
        self.n += 1
        self.prog.dma("sp", self.slots[k][:, :], self.src[s], rd=[self.srcbuf[s]], wr=[self.bufs[k]])
        return self.slots[k], self.bufs[k]


def rmsnorm(prog, banks, C, xt, xb, gcol, hT, hb, N=512):
    ps, pb = banks.next()
    g32 = C["g32"]
    for c in range(8):
        prog.add("act", lambda e, c=c: e.activation(out=C["sq"][:, c, :N], in_=xt[:, c, :N], func=AF.Square),
                 rd=[xb[c]], wr=[C["sqb"][c]])
    for c in range(8):
        prog.mm(ps[:, :N], C["ones"][:, :], C["sq"][:, c, :N], c == 0, c == 7, rd=[C["sqb"][c], C["cb"]], wr=[pb])
    prog.add("act", lambda e: e.activation(out=C["rstd"][:, :N], in_=ps[:, :N], func=AF.Sqrt, bias=C["epsc"][:, 0:1], scale=1.0 / 1024.0),
             rd=[pb, C["cb"]], wr=[C["rstdb"]])
    prog.add("dve", lambda e: e.reciprocal(out=C["rstd"][:, :N], in_=C["rstd"][:, :N]), rd=[C["rstdb"]], wr=[C["rstdb"]])
    for c in range(8):
        prog.add("dve", lambda e, c=c: e.scalar_tensor_tensor(out=hT[:, c, :N], in0=xt[:, c, :N],
                                                             scalar=g32[:, gcol + c:gcol + c + 1],
                                                             in1=C["rstd"][:, :N], op0=ALU.mult, op1=ALU.mult),
                 rd=[xb[c], C["rstdb"], C["cb"]], wr=[hb[c]])


def mlp(prog, banks, C, ws, slab0_w1, slab0_w2, hT, hb, xt, xb, N=512):
    hid, hidb = C["hid"], C["hidb"]
    for j in range(32):
        w, wb = ws.fetch(slab0_w1 + j)
        ps, pb = banks.next()
        for k in range(8):
            prog.mm(ps[:, :N], w[:, k * 128:(k + 1) * 128], hT[:, k, :N], k == 0, k == 7, rd=[wb, hb[k]], wr=[pb])
        r = j % 2
        prog.add("act", lambda e, r=r, ps=ps: e.activation(out=C["relu"][r][:, :N], in_=ps[:, :N], func=AF.Relu),
                 rd=[pb], wr=[C["relub"][r]])
        prog.add("dve", lambda e, r=r, ps=ps, j=j: e.tensor_tensor(out=hid[:, j, :N], in0=ps[:, :N], in1=C["relu"][r][:, :N],
                                                                  op=ALU.mult), rd=[pb, C["relub"][r]], wr=[hidb[j]])
    for i in range(8):
        ps, pb = banks.next()
        for kq in range(4):
            w, wb = ws.fetch(slab0_w2 + i * 4 + kq)
            for k in range(8):
                kk = kq * 8 + k
                prog.mm(ps[:, :N], w[:, k * 128:(k + 1) * 128], hid[:, kk, :N], kk == 0, kk == 31, rd=[wb, hidb[kk]], wr=[pb])
        prog.add("dve", lambda e, i=i, ps=ps: e.tensor_tensor(out=xt[:, i, :N], in0=ps[:, :N], in1=xt[:, i, :N], op=ALU.add),
                 rd=[pb, xb[i]], wr=[xb[i]])


def common_consts(prog, N=512):
    C = {}
    C["ones"] = prog.sbuf("ones", [128, 128], BF16)
    C["cb"] = prog.buf("consts")
    C["sq"] = prog.sbuf("sq", [128, 8, N], BF16)
    C["sqb"] = [prog.buf() for _ in range(8)]
    C["rstd"] = prog.sbuf("rstd", [128, N], F32)
    C["rstdb"] = prog.buf()
    C["hidb"] = [prog.buf() for _ in range(32)]
    C["relu"] = [prog.sbuf(f"relu{i}", [128, N], F32) for i in range(2)]
    C["relub"] = [prog.buf() for _ in range(2)]
    C["epsc"] = prog.sbuf("epsc", [128, 1], F32)
    prog.add("dve", lambda e: e.memset(C["ones"][:, :], 1.0), wr=[C["cb"]])
    prog.add("dve", lambda e: e.memset(C["epsc"][:, :], EPS), wr=[C["cb"]])
    return C


def build_A(prog, nc, D, banks, C):
    xT = D["xT"]
    wbf = D["wbfA"]
    wbfb = [prog.buf(f"wbfA{s}") for s in range(NSLAB_A)]
    cast_weights(prog, D["wslabA"], wbf, NSLAB_A, wbfb, "A")

    g32 = prog.sbuf("g32A", [128, 24], F32)
    cw = prog.sbuf("cwA", [128, 24], F32)
    par = prog.sbuf("parA", [128, 2], F32)
    prog.dma("sp", g32[:, :], D["gA"], wr=[C["cb"]])
    prog.dma("sp", cw[:, :], D["cwA"], wr=[C["cb"]])
    prog.dma("sp", par[:, :], D["par"], wr=[C["cb"]])
    C["g32"] = g32

    C["hid"] = prog.sbuf("hidA", [128, 32, 512], BF16)
    ws = WStream(prog, "wsA", 12, wbf, wbfb)
    xts = [prog.sbuf(f"xtA{i}", [128, 8, 512], F32) for i in range(2)]
    xbs = [[prog.buf() for _ in range(8)] for _ in range(2)]
    hT = prog.sbuf("hTA", [128, 8, 512], BF16)
    hb = [prog.buf() for _ in range(8)]
    zT = prog.sbuf("zTA", [128, 8, 512], BF16)
    zb = [prog.buf() for _ in range(8)]
    vbuf = prog.sbuf("vA", [128, 8, 514], F32)
    vb = [prog.buf() for _ in range(8)]
    csb = [prog.sbuf(f"csbA{i}", [128, 512], F32) for i in range(2)]
    csbb = [prog.buf() for _ in range(2)]
    cv = [prog.sbuf(f"cvA{i}", [128, 512], F32) for i in range(2)]
    cvb = [prog.buf() for _ in range(2)]
    xm = prog.sbuf("xmA", [128, 8, 256], F32)
    xmb = prog.buf()
    kst = prog.sbuf("kstA", [128, 8, 512], BF16)
    kstb = [prog.buf() for _ in range(8)]
    vst = prog.sbuf("vstA", [128, 4, 512], BF16)
    vstb = [prog.buf() for _ in range(4)]
    prog.add("pool", lambda e: e.memset(vbuf[:, :, :], 0.0), wr=vb)

    x1mb, kTb, rawb, vscb = D["x1m_b"], D["kT_b"], D["raw_b"], D["vsc_b"]
    xTv = xT.rearrange("(c p) t -> p c t", p=128)

    for T in range(NT):
        if "precast" in D:
            D["precast"](T)
        xt, xb = xts[T % 2], xbs[T % 2]
        prog.dma("sp", xt[:, :, :], xTv[:, :, T * 512:(T + 1) * 512], wr=xb)
        rmsnorm(prog, banks, C, xt, xb, 0, hT, hb)
        for i in range(8):
            pss = []
            for part in range(3):
                w, wb = ws.fetch(i * 3 + part)
                ps, pb = banks.next()
                for k in range(8):
                    prog.mm(ps[:, :], w[:, k * 128:(k + 1) * 128], hT[:, k, :], k == 0, k == 7, rd=[wb, hb[k]], wr=[pb])
                pss.append((ps, pb))
            (bps, bpb), (cps, cpb), (ups, upb) = pss
            r = i % 2
            prog.add("act", lambda e, r=r, cps=cps: e.copy(out=csb[r][:, :], in_=cps[:, :]), rd=[cpb], wr=[csbb[r]])
            prog.add("dve", lambda e, r=r, ups=ups, i=i: e.tensor_tensor(out=vbuf[:, i, 2:514], in0=ups[:, :], in1=csb[r][:, :], op=ALU.mult),
                     rd=[upb, csbb[r]], wr=[vb[i]])
            prog.add("dve", lambda e, r=r, i=i: e.tensor_scalar(out=cv[r][:, :], in0=vbuf[:, i, 2:514], scalar1=cw[:, i * 3 + 2:i * 3 + 3],
                                                              scalar2=None, op0=ALU.mult), rd=[vb[i], C["cb"]], wr=[cvb[r]])
            prog.add("dve", lambda e, r=r, i=i: e.scalar_tensor_tensor(out=cv[r][:, :], in0=vbuf[:, i, 1:513], scalar=cw[:, i * 3 + 1:i * 3 + 2],
                                                                     in1=cv[r][:, :], op0=ALU.mult, op1=ALU.add),
                     rd=[vb[i], cvb[r], C["cb"]], wr=[cvb[r]])
            prog.add("dve", lambda e, r=r, i=i: e.scalar_tensor_tensor(out=cv[r][:, :], in0=vbuf[:, i, 0:512], scalar=cw[:, i * 3:i * 3 + 1],
                                                                     in1=cv[r][:, :], op0=ALU.mult, op1=ALU.add),
                     rd=[vb[i], cvb[r], C["cb"]], wr=[cvb[r]])
            prog.add("dve", lambda e, r=r, i=i, bps=bps: e.tensor_tensor(out=zT[:, i, :], in0=bps[:, :], in1=cv[r][:, :], op=ALU.mult),
                     rd=[bpb, cvb[r]], wr=[zb[i]])
            prog.add("act", lambda e, i=i: e.copy(out=vbuf[:, i, 0:2], in_=vbuf[:, i, 512:514]), rd=[vb[i]], wr=[vb[i]])
        for i in range(8):
            w, wb = ws.fetch(24 + i)
            ps, pb = banks.next()
            for k in range(8):
                prog.mm(ps[:, :], w[:, k * 128:(k + 1) * 128], zT[:, k, :], k == 0, k == 7, rd=[wb, zb[k]], wr=[pb])
            prog.add("dve", lambda e, i=i, ps=ps, xt=xt: e.tensor_tensor(out=xt[:, i, :], in0=ps[:, :], in1=xt[:, i, :], op=ALU.add),
                     rd=[pb, xb[i]], wr=[xb[i]])
        rmsnorm(prog, banks, C, xt, xb, 8, hT, hb)
        mlp(prog, banks, C, ws, 32, 64, hT, hb, xt, xb)
        xv = xt.rearrange("p c (a w q) -> p c a w q", a=2, w=2)
        xmv = xm.rearrange("p c (a q) -> p c a q", a=2)
        prog.add("dve", lambda e, xv=xv: e.tensor_scalar(out=xmv, in0=xv[:, :, :, 0, :], scalar1=par[:, 0:1], scalar2=None, op0=ALU.mult),
                 rd=xb + [C["cb"]], wr=[xmb])
        prog.add("dve", lambda e, xv=xv: e.scalar_tensor_tensor(out=xmv, in0=xv[:, :, :, 1, :], scalar=par[:, 1:2], in1=xmv,
                                                               op0=ALU.mult, op1=ALU.add), rd=xb + [xmb, C["cb"]], wr=[xmb])
        prog.dma("pool", D["x1m"].rearrange("c p t -> p c t")[:, :, T * 256:(T + 1) * 256], xm[:, :, :], rd=[xmb], wr=[x1mb])
        rmsnorm(prog, banks, C, xt, xb, 16, hT, hb)
        for c in range(8):
            w, wb = ws.fetch(96 + c)
            ps, pb = banks.next()
            for k in range(8):
                prog.mm(ps[:, :], w[:, k * 128:(k + 1) * 128], hT[:, k, :], k == 0, k == 7, rd=[wb, hb[k]], wr=[pb])
            prog.add("act", lambda e, c=c, ps=ps: e.copy(out=kst[:, c, :], in_=ps[:, :]), rd=[pb], wr=[kstb[c]])
        prog.dma("pool", D["kT"].rearrange("c p t -> p c t")[:, :, T * 512:(T + 1) * 512], kst[:, 0:4, :], rd=kstb[0:4], wr=[kTb])
        prog.dma("pool", D["raw"].rearrange("c p t -> p c t")[:, :, T * 512:(T + 1) * 512], kst[:, 4:8, :], rd=kstb[4:8], wr=[rawb])
        wv = [ws.fetch(104 + j) for j in range(4)]
        for tt in range(4):
            ps, pb = banks.next()
            for k in range(8):
                w, wb = wv[k // 2]
                prog.mm(ps[:, :], hT[:, k, tt * 128:(tt + 1) * 128], w[:, (k % 2) * 512:(k % 2 + 1) * 512], k == 0, k == 7,
                        rd=[wb, hb[k]], wr=[pb])
            prog.add("act", lambda e, tt=tt, ps=ps: e.copy(out=vst[:, tt, :], in_=ps[:, :]), rd=[pb], wr=[vstb[tt]])
        prog.dma("pool", D["vsc"][:, T * 4:(T + 1) * 4, :], vst[:, :, :], rd=vstb, wr=[vscb])

    prog.barrier()
    prog.aoff = D["mark"]
    ph1z = [[prog.sbuf(f"ph1z_{i}_{h}", [128, 32, 256], BF16) for h in range(2)] for i in range(2)]
    ph1f = [prog.sbuf(f"ph1f{i}", [128, 2048], F32) for i in range(2)]
    ph1fb = [prog.buf() for _ in range(2)]
    ph1b = [prog.buf() for _ in range(2)]
    for kv in range(2):
        for h in range(2):
            prog.add("pool", lambda e, kv=kv, h=h: e.memset(ph1z[kv][h][:, :, :], 0.0), wr=[ph1b[kv]])
    for kv in range(2):
        for q4 in range(4):
            f_, fb_ = ph1f[q4 % 2], ph1fb[q4 % 2]
            prog.dma("sp", f_[:, :], D["ph1"][kv, :, q4 * 2048:(q4 + 1) * 2048], wr=[fb_])
            for h in range(2):
                eng = "dve" if h == 0 else "act"
                if h == 0:
                    prog.add("dve", lambda e, kv=kv, q4=q4, h=h, f_=f_: e.tensor_copy(
                        out=ph1z[kv][h].rearrange("p t j -> p (t j)")[h * 64:h * 64 + 64, q4 * 2048:(q4 + 1) * 2048], in_=f_[h * 64:h * 64 + 64, :]),
                        rd=[fb_, ph1b[kv]], wr=[ph1b[kv]])
                else:
                    prog.add("act", lambda e, kv=kv, q4=q4, h=h, f_=f_: e.copy(
                        out=ph1z[kv][h].rearrange("p t j -> p (t j)")[h * 64:h * 64 + 64, q4 * 2048:(q4 + 1) * 2048], in_=f_[h * 64:h * 64 + 64, :]),
                        rd=[fb_, ph1b[kv]], wr=[ph1b[kv]])
    sm32 = prog.sbuf("sm32", [128, 256 + 128 + 64], F32)
    smb = prog.buf()
    prog.dma("sp", sm32[:, 0:256], D["ph2k"], wr=[smb])
    prog.dma("sp", sm32[:, 256:384], D["ph2v"], wr=[smb])
    prog.dma("sp", sm32[:, 384:416], D["peT"][0], wr=[smb])
    prog.dma("sp", sm32[:, 416:448], D["peT"][1], wr=[smb])
    sm16 = prog.sbuf("sm16", [128, 448], BF16)
    sm16b = prog.buf()
    prog.add("dve", lambda e: e.tensor_copy(out=sm16[:, :], in_=sm32[:, :]), rd=[smb], wr=[sm16b])
    ph2k = sm16[:, 0:256].rearrange("p (k m) -> p k m", k=2)
    ph2v = sm16[:, 256:384].rearrange("p (k m) -> p k m", k=2)
    pe16 = sm16[:, 384:448].rearrange("p (v t) -> p v t", v=2)
    bpe = prog.sbuf("bpe", [128, 4], F32)
    bpeb = prog.buf()
    for kv in range(2):
        for mc in range(2):
            ps, pb = banks.next()
            for t in range(32):
                prog.mm(ps[:, 0:1], ph1z[kv][0][:, t, mc * 128:(mc + 1) * 128], pe16[:, kv, t:t + 1], t == 0, t == 31,
                        rd=[ph1b[kv], sm16b], wr=[pb])
            prog.add("dve", lambda e, kv=kv, mc=mc, ps=ps: e.tensor_copy(out=bpe[:, kv * 2 + mc:kv * 2 + mc + 1], in_=ps[:, 0:1]),
                     rd=[pb], wr=[bpeb])
    raws = [prog.sbuf(f"raws{i}", [128, 8192], BF16) for i in range(2)]
    rawsb = [prog.buf() for _ in range(2)]
    hidc = prog.sbuf("hidc", [128, 2, 512], BF16)
    hidcb = prog.buf()
    kcT = prog.sbuf("kcT", [128, 2, 512], BF16)
    kcTb = prog.buf()
    vc = prog.sbuf("vcA", [128, 4, 4, 68], BF16)
    vcb = prog.buf()
    prog.add("pool", lambda e: e.memset(hidc[:, :, :], 0.0), wr=[hidcb])
    prog.add("pool", lambda e: e.memset(kcT[:, :, :], 0.0), wr=[kcTb])
    prog.add("pool", lambda e: e.memset(vc[:, :, :, :], 1.0), wr=[vcb])
    NB = 511
    for c in range(4):
        kv, P = c // 2, c % 2
        rs, rsb = raws[c % 2], rawsb[c % 2]
        prog.dma("sp", rs[:, :], D["raw"][c], rd=[rawb], wr=[rsb])
        for half in range(2):
            g = 2 * P + half
            lo, hi = half * 64, half * 64 + 64
            for mc in range(2):
                ps, pb = banks.next()
                for t in range(32):
                    rhs = rs[:, t:t + 16 * (NB - 1) + 1:16]
                    prog.mm(ps[:, 0:NB], ph1z[kv][half][:, t, mc * 128:(mc + 1) * 128], rhs, t == 0, t == 31, rd=[ph1b[kv], rsb], wr=[pb])
                prog.add("act", lambda e, mc=mc, ps=ps, kv=kv: e.activation(out=hidc[:, mc, 0:NB], in_=ps[:, 0:NB], func=AF.Silu,
                                                                           bias=bpe[:, kv * 2 + mc:kv * 2 + mc + 1]),
                         rd=[pb, bpeb], wr=[hidcb])
            if kv == 0:
                ps, pb = banks.next()
                for mc in range(2):
                    prog.mm(ps[:, 0:NB], ph2k[:, mc, :], hidc[:, mc, 0:NB], mc == 0, mc == 1, rd=[sm16b, hidcb], wr=[pb])
                prog.add("dve", lambda e, ps=ps, lo=lo, hi=hi, P=P: e.tensor_copy(out=kcT[lo:hi, P, 0:NB], in_=ps[lo:hi, 0:NB]),
                         rd=[pb], wr=[kcTb])
            else:
                for nt in range(4):
                    ps, pb = banks.next()
                    for mc in range(2):
                        prog.mm(ps[:, 0:64], hidc[:, mc, nt * 128:(nt + 1) * 128], ph2v[:, mc, :], mc == 0, mc == 1,
                                rd=[sm16b, hidcb], wr=[pb])
                    prog.add("dve", lambda e, ps=ps, nt=nt, g=g: e.tensor_copy(out=vc[:, nt, g, 0:64], in_=ps[:, 0:64]),
                             rd=[pb], wr=[vcb])
    prog.dma("pool", D["kc"], kcT[:, :, :], rd=[kcTb], wr=[D["kc_b"]])
    prog.dma("pool", D["vc"], vc[:, :, :, :], rd=[vcb], wr=[D["vc_b"]])


NEGV = -30000.0
NSLAB_B1 = 8 + 1
NSLAB_B2 = 8 + 32 + 32


def t5_bucket_np(d):
    d = np.maximum(np.asarray(d, np.int64), 0)
    logd = np.log(np.maximum(d, 1).astype(np.float32) / np.float32(16))
    large = 16 + (logd / np.float32(math.log(8.0)) * np.float32(16)).astype(np.int32)
    large = np.minimum(large, 31)
    return np.where(d < 16, d, large).astype(np.int64)


def host_consts_B(parity):
    p = parity
    bf = ml_dtypes.bfloat16
    out = {}
    def onehot(dvals):
        L = len(dvals)
        oh = np.zeros((33, L), np.float32)
        for e, d in enumerate(dvals):
            if d >= 0:
                oh[t5_bucket_np(d), e] += 1.0
                oh[31, e] -= 1.0
            else:
                oh[32, e] = NEGV
        return oh
    out["OHn"] = onehot(np.arange(512) - 255 + 128 * p)
    out["OHc"] = onehot(np.arange(384) - 127)
    k = np.arange(128)[:, None]
    q = np.arange(128)[None, :]
    tw = []
    for off in (3, 4):
        d = q - k + 128 * (p + off)
        m = np.where(d < 512, 0.0, NEGV).astype(np.float32)
        tw.append(np.tile(m, (1, 4)))
    out["Tw"] = np.stack(tw, axis=1).astype(bf)
    W = 640
    A = np.zeros((33, W), np.float32)
    OFF = 496
    for r in range(16):
        A[r, OFF + 8 * p + 6 - r] = 1.0
    A[32, OFF + 8 * p + 7:] = 1.0
    Afull = np.zeros((128, W), np.float32)
    Afull[:33] = A
    out["Acmp"] = Afull.astype(bf)
    M = np.zeros((512, 128), np.float32)
    for j in range(128):
        for n, w in ((4 * j - 1, 1), (4 * j, 2), (4 * j + 1, 2), (4 * j + 2, 2), (4 * j + 3, 1)):
            if 0 <= n < 511:
                M[n, j] += w
    out["Mmat"] = np.ascontiguousarray(M.reshape(4, 128, 128).transpose(1, 0, 2)).reshape(128, 512).astype(bf)
    keep = np.ones((128, 256), np.float32)
    add = np.zeros((128, 256), np.float32)
    for qq in range(128):
        cbp = 2 * p + (1 if qq >= 64 else 0)
        for m in range(256):
            jp = m - 124
            if jp > cbp:
                keep[qq, m] = 0.0; add[qq, m] = -1e30
            elif jp == cbp:
                keep[qq, m] = 0.0; add[qq, m] = 2e9
            elif jp == cbp - 1:
                keep[qq, m] = 0.0; add[qq, m] = 1e9
    out["keepadd"] = np.concatenate([keep, add], axis=1)
    eye = np.eye(128, dtype=np.float32)
    out["identf"] = eye
    mats = np.concatenate([eye, eye[::-1], np.tile(eye, (1, 4))], axis=1)
    out["mats"] = mats.astype(bf)
    E = np.zeros((128, 64, 128), np.float32)
    for m in range(64):
        E[2 * m, m, 0:64] = 1.0
        E[2 * m + 1, m, 64:128] = 1.0
    out["E64"] = E.reshape(128, 8192).astype(bf)
    return out


def host_slabs_B(inp):
    def slab(W, cols):
        return np.ascontiguousarray(W[:, cols].reshape(8, 128, 128).transpose(1, 0, 2))
    wqg = inp["b_w_qg"][0]
    b1 = []
    for P in range(2):
        for hg in range(4):
            cols = np.concatenate([((2 * P) * 4 + hg) * 64 + np.arange(64), ((2 * P + 1) * 4 + hg) * 64 + np.arange(64)])
            b1.append(slab(wqg, cols).reshape(128, 1024))
    wg = wqg[:, 1024:1072].reshape(8, 128, 48).transpose(1, 0, 2).reshape(128, 384)
    gs = np.zeros((128, 1024), np.float32)
    gs[:, :384] = wg
    b1.append(gs)
    b2 = []
    wo = inp["b_w_o"][0]
    for i in range(8):
        b2.append(slab(wo, i * 128 + np.arange(128)).reshape(128, 1024))
    w1 = inp["mlp_w1"][1]
    for j in range(32):
        b2.append(slab(w1, j * 128 + np.arange(128)).reshape(128, 1024))
    w2 = inp["mlp_w2"][1]
    for i in range(8):
        for kq in range(4):
            b2.append(slab(w2[kq * 1024:(kq + 1) * 1024], i * 128 + np.arange(128)).reshape(128, 1024))
    return np.stack(b1).astype(np.float32), np.stack(b2).astype(np.float32)


def host_small_B(inp):
    def cvec(v):
        return v.reshape(8, 128).T
    g = np.stack([cvec(inp["norm_mix"][1]), cvec(inp["norm_mlp"][1]), cvec(inp["final_norm"])], axis=1)
    return dict(gB=np.ascontiguousarray(g.reshape(128, 24), dtype=np.float32),
                relb=np.ascontiguousarray(inp["rel_bias"], dtype=np.float32))


def cast_B(prog, D, part=None, nparts=1):
    if "wb1b" not in D:
        D["wb1b"] = [prog.buf(f"wbfB1_{s}") for s in range(NSLAB_B1)]
        D["wb2b"] = [prog.buf(f"wbfB2_{s}") for s in range(NSLAB_B2)]
    allslabs = [(1, s) for s in range(NSLAB_B1)] + [(2, s) for s in range(NSLAB_B2)]
    if part is not None:
        per = (len(allslabs) + nparts - 1) // nparts
        allslabs = allslabs[part * per:(part + 1) * per]
    s1 = [s for w, s in allslabs if w == 1]
    s2 = [s for w, s in allslabs if w == 2]
    if s1:
        cast_weights(prog, D["wslabB1"], D["wbfB1"], s1, D["wb1b"], "B1")
    if s2:
        cast_weights(prog, D["wslabB2"], D["wbfB2"], s2, D["wb2b"], "B2")
    return D["wb1b"], D["wb2b"]


def build_B(prog, nc, D, banks, C, NI=32):
    wb1, wb2 = D["wbfB1"], D["wbfB2"]
    if "wb1b" in D:
        wb1b, wb2b = D["wb1b"], D["wb2b"]
    else:
        wb1b, wb2b = cast_B(prog, D)
    prog.barrier()
    prog.aoff = D["mark"]

    cb = C["cb"]
    g32 = prog.sbuf("g32B", [128, 24], F32)
    prog.dma("sp", g32[:, :], D["gB"], wr=[cb])
    C["g32"] = g32
    mats = prog.sbuf("mats", [128, 768], BF16)
    prog.dma("sp", mats[:, :], D["mats"], wr=[cb])
    identb, Jm, I4 = mats[:, 0:128], mats[:, 128:256], mats[:, 256:768]
    identf = prog.sbuf("identf", [128, 128], F32)
    prog.dma("sp", identf[:, :], D["identf"], wr=[cb])
    Tw = prog.sbuf("Tw", [128, 2, 512], BF16)
    prog.dma("sp", Tw[:, :, :], D["Tw"], wr=[cb])
    Acmp = prog.sbuf("Acmp", [128, 640], BF16)
    prog.dma("sp", Acmp[:, :], D["Acmp"], wr=[cb])
    Mmat = prog.sbuf("Mmat", [128, 4, 128], BF16)
    prog.dma("sp", Mmat.rearrange("p a b -> p (a b)"), D["Mmat"], wr=[cb])
    keepadd = prog.sbuf("keepadd", [128, 512], F32)
    prog.dma("sp", keepadd[:, :], D["keepadd"], wr=[cb])
    E64 = prog.sbuf("E64", [128, 8192], BF16)
    prog.dma("sp", E64[:, :], D["E64"], wr=[cb])
    stop_at(23)
    Tn = prog.sbuf("Tn", [128, 12, 512], BF16)
    Bc = prog.sbuf("Bc", [128, 4, 512], BF16)
    mark2 = prog.aoff
    rbA = prog.sbuf("rbA", [128, 16], F32)
    ohn = prog.sbuf("ohn", [128, 512], F32)
    ohc = prog.sbuf("ohc", [128, 384], F32)
    tb = prog.buf("tblbuild")
    prog.add("dve", lambda e: e.memset(rbA[32:33, :], 1.0), wr=[tb])
    prog.dma("sp", rbA[0:32, :], D["relb"], rd=[tb], wr=[tb])
    prog.dma("sp", ohn[0:33, :], D["OHn"], wr=[tb])
    prog.dma("sp", ohc[0:33, :], D["OHc"], wr=[tb])
    vsb = prog.sbuf("vsb", [16, 896], F32)
    vsbb = prog.buf()
    ps, pb = banks.next()
    prog.mm(ps[0:16, 0:512], rbA[0:33, :], ohn[0:33, :], True, True, rd=[tb], wr=[pb])
    prog.add("dve", lambda e, ps=ps: e.tensor_copy(out=vsb[:, 0:512], in_=ps[0:16, 0:512]), rd=[pb], wr=[vsbb])
    ps, pb = banks.next()
    prog.mm(ps[0:16, 0:384], rbA[0:33, :], ohc[0:33, :], True, True, rd=[tb], wr=[pb])
    prog.add("dve", lambda e, ps=ps: e.tensor_copy(out=vsb[:, 512:896], in_=ps[0:16, 0:384]), rd=[pb], wr=[vsbb])
    vdb = prog.buf("vd")
    prog.dma("pool", D["Vd"], vsb[:, :], rd=[vsbb], wr=[vdb])
    tnb = prog.buf("Tn")
    prog.add("dve", lambda e: e.memset(Bc[:, :, :], 0.0), wr=[tnb])
    prog.add("dve", lambda e: e.memset(Bc[32:33, :, :], NEGV), rd=[tnb], wr=[tnb])
    stg = [prog.sbuf(f"tstg{i}", [128, 512], F32) for i in range(2)]
    stgb = [prog.buf() for _ in range(2)]
    Vd_t = D["Vd"].tensor
    n = 0
    for dl in range(3):
        for g in range(4):
            s, sb_ = stg[n % 2], stgb[n % 2]
            n += 1
            src = bass.AP(tensor=Vd_t, offset=(4 * g) * 896 + 128 * dl, ap=[[1, 128], [896, 4], [1, 128]])
            prog.dma("sp", s.rearrange("p (h q) -> p h q", h=4), src, rd=[vdb], wr=[sb_])
            prog.add("dve", lambda e, s=s, dl=dl, g=g: e.tensor_copy(out=Tn[:, dl * 4 + g, :], in_=s[:, :]), rd=[sb_], wr=[tnb])
    for g in range(4):
        s, sb_ = stg[n % 2], stgb[n % 2]
        n += 1
        src = bass.AP(tensor=Vd_t, offset=(4 * g) * 896 + 512, ap=[[16, 16], [896, 4], [1, 128]])
        prog.dma("sp", s[0:16, :].rearrange("p (h q) -> p h q", h=4), src, rd=[vdb], wr=[sb_])
        prog.add("dve", lambda e, s=s, g=g: e.tensor_copy(out=Bc[0:16, g, :], in_=s[0:16, :]), rd=[sb_], wr=[tnb])

    prog.barrier()
    stop_at(1)
    prog.aoff = mark2
    kcT = prog.sbuf("kcTB", [128, 2, 512], BF16)
    vc = prog.sbuf("vcB", [128, 4, 4, 68], BF16)
    kcb = prog.buf()
    prog.dma("sp", kcT[:, :, :], D["kc"], rd=[D["kc_b"]], wr=[kcb])
    prog.dma("sp", vc[:, :, :, :], D["vc"], rd=[D["vc_b"]], wr=[kcb])

    ksT = prog.sbuf("ksT", [128, 8192], BF16)
    kwT = prog.sbuf("kwT", [128, 8192], BF16)
    vs = prog.sbuf("vs", [128, 64, 2, 68], BF16)
    vw = prog.sbuf("vw", [128, 64, 2, 68], BF16)
    ksb, kwb = prog.buf("ksT"), prog.buf("kwT")
    vchain = prog.buf("vchain")
    vsb_ = [[prog.buf() for _ in range(2)] for _ in range(2)]
    vwb = [[prog.buf() for _ in range(2)] for _ in range(2)]
    prog.add("pool", lambda e: e.memset(vs[:, :, :, :], 1.0), wr=[x for r in vsb_ for x in r])
    prog.add("pool", lambda e: e.memset(vw[:, :, :, :], 1.0), wr=[x for r in vwb for x in r])

    wq = [prog.sbuf(f"wq{i}", [128, 1024], BF16) for i in range(4)]
    wgt = prog.sbuf("wgt", [128, 1024], BF16)
    wqb = prog.buf("wq")

    xq = [prog.sbuf(f"xq{i}", [128, 8, 128], F32) for i in range(2)]
    xqb = [[prog.buf() for _ in range(8)] for _ in range(2)]
    hq = prog.sbuf("hq", [128, 8, 128], BF16)
    hqb = [prog.buf() for _ in range(8)]
    NSLOT = 4
    qTz = [[prog.sbuf(f"qTz{s}_{i}", [128, 4, 128], BF16) for i in range(2)] for s in range(NSLOT)]
    qTb = [prog.buf() for _ in range(NSLOT)]
    for s_ in range(NSLOT):
        for i_ in range(2):
            prog.add("pool", lambda e, s_=s_, i_=i_: e.memset(qTz[s_][i_][:, :, :], 0.0), rd=[qTb[s_]], wr=[qTb[s_]])
    gts = [prog.sbuf(f"gts{s}", [128, 48], F32) for s in range(NSLOT)]
    gtsb = [prog.buf() for _ in range(NSLOT)]
    NPT = 4
    pts = [prog.sbuf(f"pt{i}", [128, 512], BF16) for i in range(NPT)]
    ptb = [prog.buf() for _ in range(NPT)]
    pcs = [prog.sbuf(f"pc{i}", [128, 512], BF16) for i in range(4)]
    pcb = [prog.buf() for _ in range(4)]
    accs = [[prog.sbuf(f"accs{s}_{i}", [128, 512], BF16) for i in range(3)] for s in range(2)]
    accsb = [[prog.buf() for _ in range(3)] for _ in range(2)]
    for s_ in range(2):
        for i_ in range(3):
            prog.add("pool", lambda e, s_=s_, i_=i_: e.memset(accs[s_][i_][:, :], 0.0), wr=[accsb[s_][i_]])
    obr = [[prog.sbuf(f"obr{s}_{i}", [128, 4, 66], F32) for i in range(3)] for s in range(2)]
    obrb = [[prog.buf() for _ in range(3)] for _ in range(2)]
    rden = [prog.sbuf(f"rden{s}", [128, 12], F32) for s in range(2)]
    rdenb = [[prog.buf() for _ in range(3)] for _ in range(2)]
    fbr = prog.sbuf("fbr", [128, 12], F32)
    fbrb = prog.buf()
    simp = prog.sbuf("simp", [128, 128], F32)
    score = prog.sbuf("score", [128, 128], F32)
    score2 = prog.sbuf("score2", [128, 128], F32)
    scb = prog.buf()
    mx = prog.sbuf("mx", [128, 24], F32)
    negm = prog.sbuf("negm", [128, 128], BF16)
    negmb = prog.buf()
    negmT4 = prog.sbuf("negmT4", [128, 512], BF16)
    negmTb = prog.buf()
    tmpo = prog.sbuf("tmpo", [128, 4, 64], F32)
    tmpob = prog.buf()
    otoks = [prog.sbuf(f"otok{i}", [128, 512], BF16) for i in range(2)]
    otokbs = [prog.buf() for _ in range(2)]
    oTst = prog.sbuf("oTst", [128, 4, 128], BF16)
    oTstb = prog.buf()

    misc = (banks.t[3], banks.b[3])
    acc_c = (banks.t[4], banks.b[4])
    acc_s = (banks.t[5], banks.b[5])
    acc_w = [(banks.t[6], banks.b[6]), (banks.t[7], banks.b[7])]
    rot = [0]

    def sbank():
        k = rot[0] % 3
        rot[0] += 1
        return banks.t[k], banks.b[k]

    x1mv = D["x1m"].rearrange("c p t -> p c t")
    oTv = D["oT"].rearrange("c p t -> p c t")
    kTd, vscd = D["kT"], D["vsc"]
    oTb = D["oT_b"]
    npt = [0]

    pend = []
    DEPTH = 2

    def flush_one():
        idx, n_, buf, bufb, vl, vrd, acc, accb = pend.pop(0)
        prog.mm(acc[0:65, :], vl, buf[:, :], idx == 0, idx == n_ - 1, rd=vrd + [bufb], wr=[accb])

    def flush_all():
        while pend:
            flush_one()

    def emit_branch(tiles, acc, accb, drain=False, off=0, total=None):
        n = total if total is not None else len(tiles)
        for idx0, (mms, vl, vrd, fixed) in enumerate(tiles):
            idx = idx0 + off
            ps, pb = sbank()
            for j, (l, r, rd) in enumerate(mms):
                prog.mm(ps[:, :], l, r, j == 0, j == len(mms) - 1, rd=rd, wr=[pb])
            if fixed is not None:
                buf, bufb = fixed
            else:
                s = npt[0] % NPT
                npt[0] += 1
                buf, bufb = pts[s], ptb[s]
            prog.add("act", lambda e, ps=ps, buf=buf: e.activation(out=buf[:, :], in_=ps[:, :], func=AF.Exp), rd=[pb], wr=[bufb])
            pend.append((idx, n, buf, bufb, vl, vrd, acc, accb))
            if len(pend) > DEPTH:
                flush_one()
        if drain:
            flush_all()

    def post(br, s, acc, accb, stages=(1, 2, 3), bank=None):
        post_branch(prog, br, acc, accb, accs[s], accsb[s], obr[s], obrb[s], rden[s], rdenb[s], identb, cb, bank or misc, stages)

    seqpos = {}

    def prepA(i):
        xt, xb = xq[seqpos[i] % 2], xqb[seqpos[i] % 2]
        prog.dma("sp", xt[:, :, :], x1mv[:, :, i * 128:(i + 1) * 128], rd=[D["x1m_b"]], wr=xb)
        rmsnorm_b1(prog, misc, C, xt, xb, g32, hq, hqb)

    def prepB(i):
        si = seqpos[i] % NSLOT
        ps, pb = misc
        for hg in range(4):
            for k in range(8):
                prog.mm(ps[:, hg * 128:(hg + 1) * 128], wq[hg][:, k * 128:(k + 1) * 128], hq[:, k, :], k == 0, k == 7,
                        rd=[wqb, hqb[k]], wr=[pb], skip_group_check=True)
        for gq in range(2):
            prog.add("dve", lambda e, ps=ps, gq=gq, si=si: e.tensor_scalar(out=qTz[si][gq].rearrange("p a b -> p (a b)")[gq * 64:gq * 64 + 64, :],
                                                                          in0=ps[gq * 64:gq * 64 + 64, :], scalar1=0.125, scalar2=None, op0=ALU.mult),
                     rd=[pb, qTb[si]], wr=[qTb[si]])
        ps, pb = misc
        for k in range(8):
            prog.mm(ps[:, 0:48], hq[:, k, :], wgt[:, k * 48:(k + 1) * 48], k == 0, k == 7, rd=[wqb, hqb[k]], wr=[pb])
        prog.add("act", lambda e, ps=ps, si=si: e.activation(out=gts[si][:, :], in_=ps[:, 0:48], func=AF.Exp, scale=-1.0), rd=[pb], wr=[gtsb[si]])
        prog.add("dve", lambda e, si=si: e.tensor_scalar(out=gts[si][:, :], in0=gts[si][:, :], scalar1=1.0, scalar2=None, op0=ALU.add),
                 rd=[gtsb[si]], wr=[gtsb[si]])
        prog.add("dve", lambda e, si=si: e.reciprocal(out=gts[si][:, :], in_=gts[si][:, :]), rd=[gtsb[si]], wr=[gtsb[si]])

    def head(n, P, i, gp):
        s = n % 2
        g = 2 * P + gp
        qg = qTz[seqpos[i] % NSLOT][gp][:, :, :]
        qb_ = qTb[seqpos[i] % NSLOT]
        ntc = (2 * i + 1) // 16 + 1
        cmp_tiles = []
        for nt in range(ntc):
            a0 = 128 * nt - 16 * i + 496
            cmp_tiles.append(([(kcT[:, P, nt * 128:(nt + 1) * 128], qg, [kcb, qb_]),
                               (Acmp[:, a0:a0 + 128], Bc[:, g, :], [cb, tnb])],
                              vc[:, nt, g, 0:65], [kcb], (pcs[nt], pcb[nt])))
        win_tiles = []
        for kt in range(2 * i - 4, 2 * i + 2):
            if kt < 0:
                continue
            dl = 2 * i + 1 - kt
            mms = [(kwT[:, kt * 128:(kt + 1) * 128], qg, [kwb, qb_])]
            if dl <= 2:
                mms.append((Jm, Tn[:, dl * 4 + g, :], [cb, tnb]))
            elif dl >= 4:
                mms.append((identb, Tw[:, dl - 4, :], [cb]))
            win_tiles.append((mms, vw[:, kt, gp, 0:65], [vwb[kt // 32][gp]], None))
        emit_branch(cmp_tiles, *acc_c, drain=True)
        nw0 = 0
        emit_branch(win_tiles[:nw0], *acc_w[s], off=0, total=len(win_tiles))
        post(0, s, *acc_c)
        ps, pb = misc
        for hg in range(4):
            for nt in range(ntc):
                prog.mm(ps[:, hg * 128:(hg + 1) * 128], pcs[nt][:, hg * 128:(hg + 1) * 128], Mmat[:, nt, :], nt == 0, nt == ntc - 1,
                        rd=[pcb[nt], cb], wr=[pb], skip_group_check=True)
        rd_ = rden[s]
        prog.add("dve", lambda e, ps=ps: e.tensor_scalar(out=simp[:, :], in0=ps[:, 0:128], scalar1=rd_[:, 0:1], scalar2=None, op0=ALU.mult),
                 rd=[pb, rdenb[s][0]], wr=[scb])
        for hg in range(1, 4):
            prog.add("dve", lambda e, ps=ps, hg=hg: e.scalar_tensor_tensor(out=simp[:, :], in0=ps[:, hg * 128:(hg + 1) * 128],
                                                                         scalar=rd_[:, hg:hg + 1], in1=simp[:, :], op0=ALU.mult, op1=ALU.add),
                     rd=[pb, rdenb[s][0], scb], wr=[scb])
        m0 = 124 - 4 * i
        prog.add("dve", lambda e: e.tensor_tensor(out=score[:, :], in0=simp[:, :], in1=keepadd[:, m0:m0 + 128], op=ALU.mult), rd=[scb, cb], wr=[scb])
        prog.add("dve", lambda e: e.tensor_tensor(out=score[:, :], in0=score[:, :], in1=keepadd[:, 256 + m0:256 + m0 + 128], op=ALU.add),
                 rd=[scb, cb], wr=[scb])
        prog.add("dve", lambda e: e.memset(score[:, 0:1], 3e9), rd=[scb], wr=[scb])
        prog.add("dve", lambda e: e.max(out=mx[:, 0:8], in_=score[:, :]), rd=[scb], wr=[scb])
        prog.add("dve", lambda e: e.match_replace(out=score2[:, :], in_to_replace=mx[:, 0:8], in_values=score[:, :], imm_value=-3e38), rd=[scb], wr=[scb])
        prog.add("dve", lambda e: e.max(out=mx[:, 8:16], in_=score2[:, :]), rd=[scb], wr=[scb])
        prog.add("dve", lambda e: e.tensor_reduce(out=mx[:, 16:17], in_=mx[:, 8:16], axis=mybir.AxisListType.X, op=ALU.min), rd=[scb], wr=[scb])
        prog.add("dve", lambda e: e.tensor_scalar(out=mx[:, 16:17], in0=mx[:, 16:17], scalar1=-1e29, scalar2=None, op0=ALU.max), rd=[scb], wr=[scb])
        prog.add("dve", lambda e: e.tensor_scalar(out=negm[:, :], in0=score[:, :], scalar1=mx[:, 16:17], scalar2=NEGV, op0=ALU.is_lt, op1=ALU.mult),
                 rd=[scb], wr=[negmb])
        emit_branch(win_tiles[nw0:], *acc_w[s], off=nw0, total=len(win_tiles))
        ps, pb = misc
        prog.mm(ps[:, :], negm[:, :], I4, True, True, rd=[negmb, cb], wr=[pb])
        prog.add("dve", lambda e, ps=ps: e.tensor_copy(out=negmT4[:, :], in_=ps[:, :]), rd=[pb], wr=[negmTb])

    def sel(n, P, i, gp):
        g = 2 * P + gp
        qg = qTz[seqpos[i] % NSLOT][gp][:, :, :]
        qb_ = qTb[seqpos[i] % NSLOT]
        sel_tiles = []
        for kt in range(2 * i + 2):
            dl = 2 * i + 1 - kt
            mms = [(ksT[:, kt * 128:(kt + 1) * 128], qg, [ksb, qb_])]
            if dl <= 2:
                mms.append((Jm, Tn[:, dl * 4 + g, :], [cb, tnb]))
            mms.append((E64[:, kt * 128:(kt + 1) * 128], negmT4[:, :], [negmTb, cb]))
            sel_tiles.append((mms, vs[:, kt, gp, 0:65], [vsb_[kt // 32][gp]], None))
        emit_branch(sel_tiles, *acc_s)

    def tail(n, P, i, gp):
        s = n % 2
        g = 2 * P + gp
        bk2 = sbank()
        post(2, s, *acc_w[s], stages=(1,))
        post(1, s, *acc_s, stages=(1,))
        post(2, s, *acc_w[s], stages=(2,))
        post(1, s, *acc_s, stages=(2,), bank=bk2)
        post(2, s, *acc_w[s], stages=(3,))
        post(1, s, *acc_s, stages=(3,), bank=bk2)
        gt, gtb = gts[seqpos[i] % NSLOT], gtsb[seqpos[i] % NSLOT]
        otok, otokb = otoks[seqpos[i] % 2], otokbs[seqpos[i] % 2]
        ob, obb, rd_ = obr[s], obrb[s], rden[s]
        for br in range(3):
            gsl = gt[:, g * 12 + br:g * 12 + br + 10:3]
            prog.add("dve", lambda e, br=br, gsl=gsl: e.tensor_tensor(out=fbr[:, br * 4:(br + 1) * 4], in0=gsl, in1=rd_[:, br * 4:(br + 1) * 4], op=ALU.mult),
                     rd=[gtb, rdenb[s][br]], wr=[fbrb])
        for hg in range(4):
            prog.add("dve", lambda e, hg=hg: e.tensor_scalar(out=tmpo[:, hg, :], in0=ob[0][:, hg, 0:64], scalar1=fbr[:, hg:hg + 1], scalar2=None, op0=ALU.mult),
                     rd=[obb[0], fbrb], wr=[tmpob])
            prog.add("dve", lambda e, hg=hg: e.scalar_tensor_tensor(out=tmpo[:, hg, :], in0=ob[1][:, hg, 0:64], scalar=fbr[:, 4 + hg:5 + hg],
                                                                    in1=tmpo[:, hg, :], op0=ALU.mult, op1=ALU.add), rd=[obb[1], fbrb, tmpob], wr=[tmpob])
            c0 = gp * 256 + hg * 64
            prog.add("dve", lambda e, hg=hg, c0=c0: e.scalar_tensor_tensor(out=otok[:, c0:c0 + 64], in0=ob[2][:, hg, 0:64], scalar=fbr[:, 8 + hg:9 + hg],
                                                                           in1=tmpo[:, hg, :], op0=ALU.mult, op1=ALU.add),
                     rd=[obb[2], fbrb, tmpob], wr=[otokb])
        if gp == 1:
            def tail_b():
                ps, pb = misc
                for c in range(4):
                    prog.mm(ps[:, c * 128:(c + 1) * 128], otok[:, c * 128:(c + 1) * 128], identb, True, True, rd=[otokb, cb], wr=[pb], skip_group_check=True)
                prog.add("dve", lambda e, ps=ps: e.tensor_copy(out=oTst.rearrange("p a b -> p (a b)"), in_=ps[:, :]), rd=[pb], wr=[oTstb])
                prog.dma("pool", oTv[:, P * 4:(P + 1) * 4, i * 128:(i + 1) * 128], oTst[:, :, :], rd=[oTstb], wr=[oTb])
            deferred.append(tail_b)

    nseq = 0
    deferred = []
    for P in range(2):
        for hg in range(4):
            prog.dma("sp", wq[hg][:, :], wb1[P * 4 + hg], rd=[wb1b[P * 4 + hg]], wr=[wqb])
        prog.dma("sp", wgt[:, :], wb1[8], rd=[wb1b[8]], wr=[wqb])
        prog.dma("sp", kwT[:, :], kTd[2 + P], rd=[D["kT_b"]], wr=[kwb])
        for ty, vt, vb_ in ((1, vw, vwb), (0, vs, vsb_)):
            for k2 in range(2):
                for g_ in range(2):
                    c0 = ty * 256 + P * 128 + g_ * 64
                    prog.dma("sp", vt[:, k2 * 32:(k2 + 1) * 32, g_, 0:64], vscd[:, k2 * 32:(k2 + 1) * 32, c0:c0 + 64], rd=[D["vsc_b"], vchain],
                             wr=[vb_[k2][g_], vchain])
            if ty == 1:
                prog.dma("sp", ksT[:, :], kTd[P], rd=[D["kT_b"]], wr=[ksb])
        order = []
        lo_, hi_ = 0, NI - 1
        while lo_ <= hi_:
            order.append(lo_)
            if hi_ != lo_:
                order.append(hi_)
            lo_ += 1
            hi_ -= 1
        seqpos.clear()
        seqpos.update({i: k for k, i in enumerate(order)})
        pairs = [order[k:k + 2] for k in range(0, NI, 2)]
        groups = []
        for pr in pairs:
            for gp in range(2):
                for i in pr:
                    groups.append((i, gp))
        prep_at = {}
        for k, pr in enumerate(pairs[1:]):
            base = 4 * k if len(pairs[k]) == 2 else None
            prep_at[4 * k + 1] = pr[0]
            if len(pr) > 1:
                prep_at[4 * k + 2] = pr[1]
        for i in pairs[0]:
            prepA(i)
            prepB(i)
        head(nseq, P, groups[0][0], groups[0][1])
        for idx, (i, gp) in enumerate(groups):
            n = nseq + idx
            sel(n, P, i, gp)
            while deferred:
                deferred.pop(0)()
            x = prep_at.get(idx)
            if idx + 1 < len(groups):
                i2, gp2 = groups[idx + 1]
                if x is not None:
                    prepA(x)
                head(n + 1, P, i2, gp2)
                if x is not None:
                    prepB(x)
            else:
                flush_all()
            tail(n, P, i, gp)
        while deferred:
            deferred.pop(0)()
        flush_all()
        nseq += len(groups)
    print("B1 arena used", prog.aoff)
    stop_at(7)
    prog.barrier()
    prog.aoff = D["mark"]
    g32 = prog.sbuf("g32B2", [128, 24], F32)
    prog.dma("sp", g32[:, :], D["gB"], wr=[cb])
    C["g32"] = g32
    C["hid"] = prog.sbuf("hidB", [128, 32, 512], BF16)
    ws = WStream(prog, "wsB", 12, wb2, wb2b)
    xts = [prog.sbuf(f"xtB{i}", [128, 8, 512], F32) for i in range(2)]
    xbs = [[prog.buf() for _ in range(8)] for _ in range(2)]
    oTs = [prog.sbuf(f"oTs{i}", [128, 8, 512], BF16) for i in range(2)]
    oTsb = [prog.buf() for _ in range(2)]
    hT = prog.sbuf("hTB", [128, 8, 512], BF16)
    hb = [prog.buf() for _ in range(8)]
    yo = prog.sbuf("yo", [128, 8, 512], F32)
    yob = [prog.buf() for _ in range(8)]
    outv = D["outT"].rearrange("c p t -> p c t")
    for T in range(NI // 4):
        xt, xb = xts[T % 2], xbs[T % 2]
        oT_, oTb_ = oTs[T % 2], oTsb[T % 2]
        prog.dma("sp", xt[:, :, :], x1mv[:, :, T * 512:(T + 1) * 512], rd=[D["x1m_b"]], wr=xb)
        prog.dma("sp", oT_[:, :, :], oTv[:, :, T * 512:(T + 1) * 512], rd=[oTb], wr=[oTb_])
        for i in range(8):
            w, wb = ws.fetch(i)
            ps, pb = banks.next()
            for k in range(8):
                prog.mm(ps[:, :], w[:, k * 128:(k + 1) * 128], oT_[:, k, :], k == 0, k == 7, rd=[wb, oTb_], wr=[pb])
            prog.add("dve", lambda e, i=i, ps=ps, xt=xt: e.tensor_tensor(out=xt[:, i, :], in0=ps[:, :], in1=xt[:, i, :], op=ALU.add),
                     rd=[pb, xb[i]], wr=[xb[i]])
        rmsnorm(prog, banks, C, xt, xb, 8, hT, hb)
        mlp(prog, banks, C, ws, 8, 40, hT, hb, xt, xb)
        ps, pb = banks.next()
        for c in range(8):
            prog.add("act", lambda e, c=c, xt=xt: e.activation(out=C["sq"][:, c, :], in_=xt[:, c, :], func=AF.Square), rd=[xb[c]], wr=[C["sqb"][c]])
        for c in range(8):
            prog.mm(ps[:, :], C["ones"][:, :], C["sq"][:, c, :], c == 0, c == 7, rd=[C["sqb"][c], cb], wr=[pb])
        prog.add("act", lambda e, ps=ps: e.activation(out=C["rstd"][:, :], in_=ps[:, :], func=AF.Sqrt, bias=C["epsc"][:, 0:1], scale=1.0 / 1024.0),
                 rd=[pb, cb], wr=[C["rstdb"]])
        prog.add("dve", lambda e: e.reciprocal(out=C["rstd"][:, :], in_=C["rstd"][:, :]), rd=[C["rstdb"]], wr=[C["rstdb"]])
        for c in range(8):
            prog.add("dve", lambda e, c=c, xt=xt: e.scalar_tensor_tensor(out=yo[:, c, :], in0=xt[:, c, :], scalar=g32[:, 16 + c:17 + c], in1=C["rstd"][:, :],
                                                                        op0=ALU.mult, op1=ALU.mult), rd=[xb[c], C["rstdb"], cb], wr=[yob[c]])
        prog.dma("pool", outv[:, :, T * 512:(T + 1) * 512], yo[:, :, :], rd=yob, wr=[D["out_b"]])


class StopBuild(Exception):
    pass


STOP = int(os.environ.get("B_STOP", "0"))


GPSEL = int(os.environ.get("B_GP", "0"))


def stop_at(n, gp=None):
    if STOP == n and (gp is None or gp == GPSEL):
        raise StopBuild()


def rmsnorm_b1(prog, misc, C, xt, xb, g32, hT, hb):
    ps, pb = misc
    sq = C["sq"]
    prog.add("dve", lambda e: e.tensor_tensor(out=sq[:, :, 0:128], in0=xt[:, :, :], in1=xt[:, :, :], op=ALU.mult), rd=xb, wr=C["sqb"])
    for c in range(8):
        prog.mm(ps[:, 0:128], C["ones"][:, :], sq[:, c, 0:128], c == 0, c == 7, rd=[C["sqb"][c], C["cb"]], wr=[pb])
    rstd = C["rstd"]
    prog.add("act", lambda e: e.activation(out=rstd[:, 0:128], in_=ps[:, 0:128], func=AF.Ln, bias=C["epsc"][:, 0:1], scale=1.0 / 1024.0),
             rd=[pb, C["cb"]], wr=[C["rstdb"]])
    prog.add("act", lambda e: e.activation(out=rstd[:, 0:128], in_=rstd[:, 0:128], func=AF.Exp, scale=-0.5), rd=[C["rstdb"]], wr=[C["rstdb"]])
    for c in range(8):
        prog.add("dve", lambda e, c=c: e.scalar_tensor_tensor(out=hT[:, c, :], in0=xt[:, c, :], scalar=g32[:, c:c + 1], in1=rstd[:, 0:128],
                                                             op0=ALU.mult, op1=ALU.mult), rd=[xb[c], C["rstdb"], C["cb"]], wr=[hb[c]])


class banks_misc:
    def __init__(self, banks):
        self.banks = banks

    def next(self):
        return self.banks.t[7], self.banks.b[7]


def post_branch(prog, br, acc, accb, accs, accsb, obr, obrb, rden, rdenb, identb, cb, misc, stages=(1, 2, 3)):
    ps, pb = misc
    if 1 in stages:
        prog.add("dve", lambda e: e.tensor_copy(out=accs[br][0:65, :], in_=acc[0:65, :]), rd=[accb], wr=[accsb[br]])
    if 2 in stages:
        for hg in range(4):
            prog.mm(ps[:, hg * 66:hg * 66 + 65], accs[br][:, hg * 128:(hg + 1) * 128], identb[:, 0:65], True, True,
                    rd=[accsb[br], cb], wr=[pb], skip_group_check=True)
    if 3 in stages:
        prog.add("dve", lambda e, ps=ps: e.tensor_copy(out=obr[br].rearrange("p a b -> p (a b)")[:, 0:263], in_=ps[:, 0:263]), rd=[pb], wr=[obrb[br]])
        prog.add("dve", lambda e: e.tensor_scalar(out=rden[:, br * 4:(br + 1) * 4], in0=obr[br][:, :, 64], scalar1=1e-30, scalar2=None, op0=ALU.add),
                 rd=[obrb[br]], wr=[rdenb[br]])
        prog.add("dve", lambda e: e.reciprocal(out=rden[:, br * 4:(br + 1) * 4], in_=rden[:, br * 4:(br + 1) * 4]), rd=[rdenb[br]], wr=[rdenb[br]])

import time as _time
from concourse.bass_utils import run_bass_kernel_spmd

FUSED = True
NCORES = 8


def _declare(nc, prog, mode):
    D = {}

    def din(name, shape, dt=F32):
        D[name] = nc.dram_tensor(name, list(shape), dt, kind="ExternalInput").ap()

    def dten(name, shape, dt, kind):
        D[name] = nc.dram_tensor(name, list(shape), dt, kind=kind).ap()

    a_out_kind = {"A": "ExternalOutput", "AB": os.environ.get("AB_KIND", "Internal")}
    if mode in ("A", "AB"):
        din("xT", [1024, 8192]); din("wslabA", [NSLAB_A, 128, 1024]); din("gA", [128, 24]); din("cwA", [128, 24]); din("par", [128, 2])
        din("ph1", [2, 128, 8192]); din("ph2k", [128, 256]); din("ph2v", [128, 128]); din("peT", [2, 128, 32])
        dten("wbfA", [NSLAB_A, 128, 1024], BF16, "Internal")
        dten("raw", [4, 128, 8192], BF16, "Internal")
        k = a_out_kind[mode]
        dten("x1m", [8, 128, 4096], F32, k); dten("kT", [4, 128, 8192], BF16, k); dten("vsc", [128, 64, 512], BF16, k)
        dten("kc", [128, 2, 512], BF16, k); dten("vc", [128, 4, 4, 68], BF16, k)
    if mode == "B":
        din("x1m", [8, 128, 4096]); din("kT", [4, 128, 8192], BF16); din("vsc", [128, 64, 512], BF16)
        din("kc", [128, 2, 512], BF16); din("vc", [128, 4, 4, 68], BF16)
    if mode in ("B", "AB"):
        din("wslabB1", [NSLAB_B1, 128, 1024]); din("wslabB2", [NSLAB_B2, 128, 1024])
        din("gB", [128, 24]); din("relb", [32, 16]); din("OHn", [33, 512]); din("OHc", [33, 384])
        din("Tw", [128, 2, 512], BF16); din("Acmp", [128, 640], BF16); din("Mmat", [128, 512], BF16)
        din("keepadd", [128, 512]); din("identf", [128, 128]); din("mats", [128, 768], BF16); din("E64", [128, 8192], BF16)
        dten("wbfB1", [NSLAB_B1, 128, 1024], BF16, "Internal"); dten("wbfB2", [NSLAB_B2, 128, 1024], BF16, "Internal")
        dten("Vd", [16, 896], F32, "Internal"); dten("oT", [8, 128, 4096], BF16, "Internal")
        dten("outT", [8, 128, 4096], F32, "ExternalOutput")
    for n in ("x1m", "kT", "raw", "vsc", "kc", "vc", "oT", "out"):
        D[n + "_b"] = prog.buf(n, multi=True)
    return D


def _build(mode, NI=32):
    nc = bass.Bass("TRN2", target_bir_lowering=False)
    prog = Prog(nc)
    D = _declare(nc, prog, mode)
    prog.init_arena(206 * 1024)
    banks = Banks(prog)
    C = common_consts(prog)
    D["mark"] = prog.aoff
    outs = []
    if mode == "AB":
        def _precast(T):
            if 1 <= T <= 14:
                cast_B(prog, D, part=T - 1, nparts=14)
        D["precast"] = _precast
    if mode in ("A", "AB"):
        build_A(prog, nc, D, banks, C)
        outs += ["x1m", "kT", "vsc", "kc", "vc"]
    if mode == "AB":
        prog.barrier()
        prog.aoff = D["mark"]
    if mode in ("B", "AB") and not os.environ.get("SKIP_B"):
        try:
            build_B(prog, nc, D, banks, C, NI=NI)
        except StopBuild:
            prog.barrier()
        outs = ["out"] if mode == "AB" else ["out"]
    if os.environ.get("SKIP_B") or os.environ.get("B_STOP"):
        outs = ["x1m", "kT", "vsc", "kc", "vc"]
    prog.add("sp", None, rd=[D[n + "_b"] for n in outs])
    prog.emit()
    return nc, prog


def _maps_A(inp):
    slabs = host_slabs_A(inp)
    maps = []
    for c in range(NCORES):
        b, p = c // 2, c % 2
        m = dict(xT=np.ascontiguousarray(inp["x"][b].T), wslabA=slabs)
        m.update(host_small_A(inp, p))
        maps.append(m)
    return maps


def _maps_B(inp):
    b1, b2 = host_slabs_B(inp)
    small = host_small_B(inp)
    consts = [host_consts_B(0), host_consts_B(1)]
    maps = []
    for c in range(NCORES):
        m = dict(wslabB1=b1, wslabB2=b2)
        m.update(small)
        m.update(consts[c % 2])
        maps.append(m)
    return maps


def _assemble(results):
    out = np.empty((4, 8192, 1024), np.float32)
    for c in range(NCORES):
        b, p = c // 2, c % 2
        o = np.asarray(results[c]["outT"])
        tok = o.transpose(2, 0, 1).reshape(32, 128, 1024)
        out[b].reshape(32, 2, 128, 1024)[:, p] = tok
    return out


def kernel(**inputs):
    inp = {k: np.asarray(v) for k, v in inputs.items()}
    ids = list(range(NCORES))
    if FUSED:
        nc, prog = _build("AB")
        ma, mb = _maps_A(inp), _maps_B(inp)
        maps = [dict(a, **b) for a, b in zip(ma, mb)]
        res = run_bass_kernel_spmd(nc, maps, core_ids=ids)
        return _assemble(res.results)
    ncA, _ = _build("A")
    resA = run_bass_kernel_spmd(ncA, _maps_A(inp), core_ids=ids)
    ncB, _ = _build("B")
    mb = _maps_B(inp)
    for c in range(NCORES):
        for k in ("x1m", "kT", "vsc", "kc", "vc"):
            mb[c][k] = np.asarray(resA.results[c][k])
    resB = run_bass_kernel_spmd(ncB, mb, core_ids=ids)
    return _assemble(resB.results)
```

```python
import os
import math
import contextlib
import numpy as np
import ml_dtypes
import concourse.bass as bass
import concourse.mybir as mybir


F32 = mybir.dt.float32
BF16 = mybir.dt.bfloat16
AF = mybir.ActivationFunctionType
ALU = mybir.AluOpType

SEM_CAP = 30000
DMA_RING = 8


class Buf:
    __slots__ = ("name", "w", "r", "multi", "ws")

    def __init__(self, name="", multi=False):
        self.name = name
        self.w = None
        self.r = []
        self.multi = multi
        self.ws = []


class Op:
    __slots__ = ("eng", "fn", "deps", "dma", "sig", "sem", "val", "prev_ring")

    def __init__(self, eng, fn, deps, dma):
        self.eng = eng
        self.fn = fn
        self.deps = deps
        self.dma = dma
        self.sig = False
        self.sem = None
        self.val = 0
        self.prev_ring = None


class Prog:
    ENGS = ("pe", "act", "dve", "pool", "sp")

    def __init__(self, nc):
        self.nc = nc
        self.ops = []
        self.stack = contextlib.ExitStack()
        self.nbuf = 0

    def init_arena(self, nbytes):
        self.arena = self.stack.enter_context(self.nc.sbuf_tensor("sb_arena", [128, nbytes // 2], BF16))
        self.arena_bytes = nbytes
        self.aoff = 0

    def sbuf(self, name, shape, dtype):
        shape = list(shape)
        n = 1
        for d in shape[1:]:
            n *= d
        esz = 4 if dtype == F32 else 2
        nb = (n * esz + 31) // 32 * 32
        if self.aoff + nb > self.arena_bytes:
            raise RuntimeError(f"arena overflow allocating {name} {shape}: need {nb}, used {self.aoff}/{self.arena_bytes}")
        v = self.arena[:, self.aoff // 2:self.aoff // 2 + n * esz // 2]
        self.aoff += nb
        if esz == 4:
            v = v.bitcast(F32)
        if len(shape) > 2:
            names = [f"d{i}" for i in range(len(shape) - 1)]
            pat = "p (" + " ".join(names) + ") -> p " + " ".join(names)
            v = v.rearrange(pat, **{nm: d for nm, d in zip(names, shape[1:])})
        if shape[0] < 128:
            v = v[0:shape[0]]
        return v

    def barrier(self):
        last = {}
        dmas = []
        for i, op in enumerate(self.ops):
            if op.dma:
                dmas.append(i)
            elif op.fn is not None:
                last[op.eng] = i
        start = getattr(self, "_bar_from", 0)
        dmas = [i for i in dmas if i >= start]
        deps = set(last.values()) | set(dmas)
        for e in self.ENGS:
            self.ops.append(Op(e, None, set(deps), False))
        self._bar_from = len(self.ops)

    def psum(self, name, shape, dtype=F32):
        t = self.stack.enter_context(self.nc.psum_tensor("ps_" + name, list(shape), dtype))
        return t

    def buf(self, name="", multi=False):
        self.nbuf += 1
        return Buf(name or f"b{self.nbuf}", multi)

    def add(self, eng, fn, rd=(), wr=(), dma=False):
        deps = set()
        for b in rd:
            if b.multi:
                deps.update(b.ws)
            elif b.w is not None:
                deps.add(b.w)
        for b in wr:
            if (not b.multi) and b.w is not None:
                deps.add(b.w)
            deps.update(b.r)
        idx = len(self.ops)
        self.ops.append(Op(eng, fn, deps, dma))
        for b in rd:
            b.r.append(idx)
        for b in wr:
            if b.multi:
                b.ws.append(idx)
            else:
                b.w = idx
            b.r = []
        return idx

    def dma(self, q, out, in_, rd=(), wr=(), **kw):
        return self.add(q, lambda e: e.dma_start(out=out, in_=in_, **kw), rd, wr, dma=True)

    def mm(self, out, lhsT, rhs, start, stop, rd=(), wr=(), **kw):
        return self.add("pe", lambda e: e.matmul(out, lhsT, rhs, start=start, stop=stop, **kw), rd, wr)

    def emit(self):
        nc = self.nc
        ops = self.ops
        for i, op in enumerate(ops):
            keep = set()
            best = {}
            for d in op.deps:
                o = ops[d]
                if o.dma:
                    keep.add(d)
                else:
                    if o.eng == op.eng and op.eng == "pe" and not op.dma:
                        continue
                    if o.eng not in best or best[o.eng] < d:
                        best[o.eng] = d
            keep.update(best.values())
            op.deps = keep
            for d in keep:
                ops[d].sig = True
        ring_cnt = {}
        ring_last = {}
        for i, op in enumerate(ops):
            if op.dma:
                n = ring_cnt.get(op.eng, 0)
                ring_cnt[op.eng] = n + 1
                slot = n % DMA_RING
                key = (op.eng, slot)
                op.prev_ring = ring_last.get(key)
                ring_last[key] = i
                op.sem = ("dma", op.eng, slot)
                op.val = 16 * (n // DMA_RING + 1)
                op.sig = True
        cnt = {e: 0 for e in self.ENGS}
        for op in ops:
            if op.sig and not op.dma:
                n = cnt[op.eng]
                cnt[op.eng] = n + 1
                op.sem = ("eng", op.eng, n // SEM_CAP)
                op.val = n % SEM_CAP + 1
        semnames = []
        for op in ops:
            if op.sig and op.sem not in semnames:
                semnames.append(op.sem)
        sems = {}
        for sn in semnames:
            sems[sn] = self.stack.enter_context(nc.semaphore("s_" + "_".join(str(x) for x in sn)))
        self.nsems = len(sems)

        per_eng = {e: [] for e in self.ENGS}
        for i, op in enumerate(ops):
            per_eng[op.eng].append(i)

        def run_engine(ename, eh):
            waited = {}
            for i in per_eng[ename]:
                op = ops[i]
                need = []
                for d in op.deps:
                    o = ops[d]
                    need.append((o.sem, o.val))
                if op.prev_ring is not None:
                    o = ops[op.prev_ring]
                    need.append((o.sem, o.val))
                for sn, v in need:
                    if waited.get(sn, 0) >= v:
                        continue
                    eh.wait_ge(sems[sn], v)
                    waited[sn] = v
                ins = op.fn(eh) if op.fn is not None else None
                if op.sig:
                    if ins is None:
                        raise RuntimeError("signaling op returned None")
                    ins.then_inc(sems[op.sem], 16 if op.dma else 1)

        with nc.Block() as block:
            @block.sync
            def _(e):
                run_engine("sp", e)

            @block.scalar
            def _(e):
                run_engine("act", e)

            @block.vector
            def _(e):
                run_engine("dve", e)

            @block.gpsimd
            def _(e):
                run_engine("pool", e)

            @block.tensor
            def _(e):
                run_engine("pe", e)

    def close(self):
        self.stack.close()


EPS = 1e-6
S = 8192
NT = S // 512
NSLAB_A = 24 + 8 + 32 + 32 + 8 + 4


def host_slabs_A(inp):
    def slab(W, cols):
        return np.ascontiguousarray(W[:, cols].reshape(8, 128, 128).transpose(1, 0, 2))
    out = []
    w_in = inp["a_w_in"][0]
    for i in range(8):
        for part in range(3):
            out.append(slab(w_in, part * 1024 + i * 128 + np.arange(128)))
    w_out = inp["a_w_out"][0]
    for i in range(8):
        out.append(slab(w_out, i * 128 + np.arange(128)))
    w1 = inp["mlp_w1"][0]
    for j in range(32):
        out.append(slab(w1, j * 128 + np.arange(128)))
    w2 = inp["mlp_w2"][0]
    for i in range(8):
        for kq in range(4):
            out.append(slab(w2[kq * 1024:(kq + 1) * 1024], i * 128 + np.arange(128)))
    wkv = inp["w_kv"]
    def kvcols(ty, g):
        return ty * 256 + g * 64 + np.arange(64)
    for ty in (2, 4, 0, 1):
        for P in range(2):
            out.append(slab(wkv, np.concatenate([kvcols(ty, 2 * P), kvcols(ty, 2 * P + 1)])))
    vcols = np.concatenate([3 * 256 + np.arange(256), 5 * 256 + np.arange(256)])
    wv = wkv[:, vcols].reshape(8, 128, 512)
    for j in range(4):
        out.append(np.ascontiguousarray(wv[2 * j:2 * j + 2].transpose(1, 0, 2)).reshape(128, 8, 128))
    return np.stack(out).reshape(len(out), 128, 1024).astype(np.float32)


def host_small_A(inp, parity):
    def cvec(v):
        return v.reshape(8, 128).T
    g = np.stack([cvec(inp["norm_mix"][0]), cvec(inp["norm_mlp"][0]), cvec(inp["kv_norm"])], axis=1)
    cw = inp["a_conv_w"][0]
    cwl = np.stack([cvec(cw[k]) for k in range(3)], axis=2)
    par = np.zeros((128, 2), np.float32)
    par[:, 0] = 1.0 - parity
    par[:, 1] = parity
    def phi1(w):
        a = w.reshape(32, 64, 256).transpose(1, 0, 2)
        return np.concatenate([a, a], axis=0).reshape(128, 32 * 256)
    ph1 = np.stack([phi1(inp["phi_k_w1"]), phi1(inp["phi_v_w1"])])
    w2k = inp["phi_k_w2"].reshape(2, 128, 64).transpose(1, 0, 2)
    ph2k = np.concatenate([w2k, w2k], axis=2)
    ph2v = inp["phi_v_w2"].reshape(2, 128, 64).transpose(1, 0, 2)
    def peT(pe):
        return np.concatenate([pe.T, pe.T], axis=0)
    pe = np.stack([peT(inp["cmp_pe_k"]), peT(inp["cmp_pe_v"])])
    f = lambda a: np.ascontiguousarray(a, dtype=np.float32)
    return dict(gA=f(g.reshape(128, 24)), cwA=f(cwl.reshape(128, 24)), par=f(par),
                ph1=f(ph1), ph2k=f(ph2k.reshape(128, 256)), ph2v=f(ph2v.reshape(128, 128)), peT=f(pe))


class Banks:
    def __init__(self, prog, n=8):
        self.t = [prog.psum(f"bank{i}", [128, 512], F32) for i in range(n)]
        self.b = [prog.buf(f"bank{i}") for i in range(n)]
        self.i = 0
        self.n = n

    def next(self):
        k = self.i % self.n
        self.i += 1
        return self.t[k], self.b[k]


def cast_weights(prog, src, dst, nslab, dstbuf, tag):
    nst = 3
    key = ("cast_stage", prog.aoff_epoch if hasattr(prog, "aoff_epoch") else 0)
    if getattr(prog, "_cast_stage_key", None) != id(prog.ops) or getattr(prog, "_cast_stage_mark", None) is None or prog._cast_stage_mark > prog.aoff:
        st32 = [prog.sbuf(f"{tag}c32_{i}", [128, 1024], F32) for i in range(nst)]
        st16 = [prog.sbuf(f"{tag}c16_{i}", [128, 1024], BF16) for i in range(nst)]
        b32 = [prog.buf() for _ in range(nst)]
        b16 = [prog.buf() for _ in range(nst)]
        prog._cast_stage = (st32, st16, b32, b16)
        prog._cast_stage_key = id(prog.ops)
        prog._cast_stage_mark = prog.aoff
    st32, st16, b32, b16 = prog._cast_stage
    slabs = list(nslab) if not isinstance(nslab, int) else list(range(nslab))
    cnt = getattr(prog, "_cast_cnt", 0)

    def load(j):
        s = slabs[j]
        k = (cnt + j) % nst
        prog.dma("pool", st32[k][:, :], src[s], wr=[b32[k]])

    def cast_store(j):
        s = slabs[j]
        k = (cnt + j) % nst
        r = (cnt + j) % 3
        if r == 0:
            prog.add("dve", lambda e, k=k: e.tensor_copy(out=st16[k][:, :], in_=st32[k][:, :]), rd=[b32[k]], wr=[b16[k]])
        elif r == 1:
            prog.add("act", lambda e, k=k: e.copy(out=st16[k][:, :], in_=st32[k][:, :]), rd=[b32[k]], wr=[b16[k]])
        else:
            prog.add("pool", lambda e, k=k: e.tensor_copy(out=st16[k][:, :], in_=st32[k][:, :]), rd=[b32[k]], wr=[b16[k]])
        prog.dma("pool", dst[s], st16[k][:, :], rd=[b16[k]], wr=[dstbuf[s]])

    LA = 2
    n = len(slabs)
    for j in range(n + LA):
        if j < n:
            load(j)
        if j - LA >= 0:
            cast_store(j - LA)
    prog._cast_cnt = cnt + n


class WStream:
    def __init__(self, prog, name, nslots, src, srcbuf):
        self.prog = prog
        self.slots = [prog.sbuf(f"{name}_{i}", [128, 1024], BF16) for i in range(nslots)]
        self.bufs = [prog.buf(f"{name}_{i}") for i in range(nslots)]
        self.n = 0
        self.src = src
        self.srcbuf = srcbuf

    def fetch(self, s):
        k = self.n % len(self.slots)
        self.n += 1
        self.prog.dma("sp", self.slots[k][:, :], self.src[s], rd=[self.srcbuf[s]], wr=[self.bufs[k]])
        return self.slots[k], self.bufs[k]


def rmsnorm(prog, banks, C, xt, xb, gcol, hT, hb, N=512):
    ps, pb = banks.next()
    g32 = C["g32"]
    for c in range(8):
        prog.add("act", lambda e, c=c: e.activation(out=C["sq"][:, c, :N], in_=xt[:, c, :N], func=AF.Square),
                 rd=[xb[c]], wr=[C["sqb"][c]])
    for c in range(8):
        prog.mm(ps[:, :N], C["ones"][:, :], C["sq"][:, c, :N], c == 0, c == 7, rd=[C["sqb"][c], C["cb"]], wr=[pb])
    prog.add("act", lambda e: e.activation(out=C["rstd"][:, :N], in_=ps[:, :N], func=AF.Sqrt, bias=C["epsc"][:, 0:1], scale=1.0 / 1024.0),
             rd=[pb, C["cb"]], wr=[C["rstdb"]])
    prog.add("dve", lambda e: e.reciprocal(out=C["rstd"][:, :N], in_=C["rstd"][:, :N]), rd=[C["rstdb"]], wr=[C["rstdb"]])
    for c in range(8):
        prog.add("dve", lambda e, c=c: e.scalar_tensor_tensor(out=hT[:, c, :N], in0=xt[:, c, :N],
                                                             scalar=g32[:, gcol + c:gcol + c + 1],
                                                             in1=C["rstd"][:, :N], op0=ALU.mult, op1=ALU.mult),
                 rd=[xb[c], C["rstdb"], C["cb"]], wr=[hb[c]])


def mlp(prog, banks, C, ws, slab0_w1, slab0_w2, hT, hb, xt, xb, N=512):
    hid, hidb = C["hid"], C["hidb"]
    for j in range(32):
        w, wb = ws.fetch(slab0_w1 + j)
        ps, pb = banks.next()
        for k in range(8):
            prog.mm(ps[:, :N], w[:, k * 128:(k + 1) * 128], hT[:, k, :N], k == 0, k == 7, rd=[wb, hb[k]], wr=[pb])
        r = j % 2
        prog.add("act", lambda e, r=r, ps=ps: e.activation(out=C["relu"][r][:, :N], in_=ps[:, :N], func=AF.Relu),
                 rd=[pb], wr=[C["relub"][r]])
        prog.add("dve", lambda e, r=r, ps=ps, j=j: e.tensor_tensor(out=hid[:, j, :N], in0=ps[:, :N], in1=C["relu"][r][:, :N],
                                                                  op=ALU.mult), rd=[pb, C["relub"][r]], wr=[hidb[j]])
    for i in range(8):
        ps, pb = banks.next()
        for kq in range(4):
            w, wb = ws.fetch(slab0_w2 + i * 4 + kq)
            for k in range(8):
                kk = kq * 8 + k
                prog.mm(ps[:, :N], w[:, k * 128:(k + 1) * 128], hid[:, kk, :N], kk == 0, kk == 31, rd=[wb, hidb[kk]], wr=[pb])
        prog.add("dve", lambda e, i=i, ps=ps: e.tensor_tensor(out=xt[:, i, :N], in0=ps[:, :N], in1=xt[:, i, :N], op=ALU.add),
                 rd=[pb, xb[i]], wr=[xb[i]])


def common_consts(prog, N=512):
    C = {}
    C["ones"] = prog.sbuf("ones", [128, 128], BF16)
    C["cb"] = prog.buf("consts")
    C["sq"] = prog.sbuf("sq", [128, 8, N], BF16)
    C["sqb"] = [prog.buf() for _ in range(8)]
    C["rstd"] = prog.sbuf("rstd", [128, N], F32)
    C["rstdb"] = prog.buf()
    C["hidb"] = [prog.buf() for _ in range(32)]
    C["relu"] = [prog.sbuf(f"relu{i}", [128, N], F32) for i in range(2)]
    C["relub"] = [prog.buf() for _ in range(2)]
    C["epsc"] = prog.sbuf("epsc", [128, 1], F32)
    prog.add("dve", lambda e: e.memset(C["ones"][:, :], 1.0), wr=[C["cb"]])
    prog.add("dve", lambda e: e.memset(C["epsc"][:, :], EPS), wr=[C["cb"]])
    return C


def build_A(prog, nc, D, banks, C):
    xT = D["xT"]
    wbf = D["wbfA"]
    wbfb = [prog.buf(f"wbfA{s}") for s in range(NSLAB_A)]
    cast_weights(prog, D["wslabA"], wbf, NSLAB_A, wbfb, "A")

    g32 = prog.sbuf("g32A", [128, 24], F32)
    cw = prog.sbuf("cwA", [128, 24], F32)
    par = prog.sbuf("parA", [128, 2], F32)
    prog.dma("sp", g32[:, :], D["gA"], wr=[C["cb"]])
    prog.dma("sp", cw[:, :], D["cwA"], wr=[C["cb"]])
    prog.dma("sp", par[:, :], D["par"], wr=[C["cb"]])
    C["g32"] = g32

    C["hid"] = prog.sbuf("hidA", [128, 32, 512], BF16)
    ws = WStream(prog, "wsA", 12, wbf, wbfb)
    xts = [prog.sbuf(f"xtA{i}", [128, 8, 512], F32) for i in range(2)]
    xbs = [[prog.buf() for _ in range(8)] for _ in range(2)]
    hT = prog.sbuf("hTA", [128, 8, 512], BF16)
    hb = [prog.buf() for _ in range(8)]
    zT = prog.sbuf("zTA", [128, 8, 512], BF16)
    zb = [prog.buf() for _ in range(8)]
    vbuf = prog.sbuf("vA", [128, 8, 514], F32)
    vb = [prog.buf() for _ in range(8)]
    csb = [prog.sbuf(f"csbA{i}", [128, 512], F32) for i in range(2)]
    csbb = [prog.buf() for _ in range(2)]
    cv = [prog.sbuf(f"cvA{i}", [128, 512], F32) for i in range(2)]
    cvb = [prog.buf() for _ in range(2)]
    xm = prog.sbuf("xmA", [128, 8, 256], F32)
    xmb = prog.buf()
    kst = prog.sbuf("kstA", [128, 8, 512], BF16)
    kstb = [prog.buf() for _ in range(8)]
    vst = prog.sbuf("vstA", [128, 4, 512], BF16)
    vstb = [prog.buf() for _ in range(4)]
    prog.add("pool", lambda e: e.memset(vbuf[:, :, :], 0.0), wr=vb)

    x1mb, kTb, rawb, vscb = D["x1m_b"], D["kT_b"], D["raw_b"], D["vsc_b"]
    xTv = xT.rearrange("(c p) t -> p c t", p=128)

    for T in range(NT):
        if "precast" in D:
            D["precast"](T)
        xt, xb = xts[T % 2], xbs[T % 2]
        prog.dma("sp", xt[:, :, :], xTv[:, :, T * 512:(T + 1) * 512], wr=xb)
        rmsnorm(prog, banks, C, xt, xb, 0, hT, hb)
        for i in range(8):
            pss = []
            for part in range(3):
                w, wb = ws.fetch(i * 3 + part)
                ps, pb = banks.next()
                for k in range(8):
                    prog.mm(ps[:, :], w[:, k * 128:(k + 1) * 128], hT[:, k, :], k == 0, k == 7, rd=[wb, hb[k]], wr=[pb])
                pss.append((ps, pb))
            (bps, bpb), (cps, cpb), (ups, upb) = pss
            r = i % 2
            prog.add("act", lambda e, r=r, cps=cps: e.copy(out=csb[r][:, :], in_=cps[:, :]), rd=[cpb], wr=[csbb[r]])
            prog.add("dve", lambda e, r=r, ups=ups, i=i: e.tensor_tensor(out=vbuf[:, i, 2:514], in0=ups[:, :], in1=csb[r][:, :], op=ALU.mult),
                     rd=[upb, csbb[r]], wr=[vb[i]])
            prog.add("dve", lambda e, r=r, i=i: e.tensor_scalar(out=cv[r][:, :], in0=vbuf[:, i, 2:514], scalar1=cw[:, i * 3 + 2:i * 3 + 3],
                                                              scalar2=None, op0=ALU.mult), rd=[vb[i], C["cb"]], wr=[cvb[r]])
            prog.add("dve", lambda e, r=r, i=i: e.scalar_tensor_tensor(out=cv[r][:, :], in0=vbuf[:, i, 1:513], scalar=cw[:, i * 3 + 1:i * 3 + 2],
                                                                     in1=cv[r][:, :], op0=ALU.mult, op1=ALU.add),
                     rd=[vb[i], cvb[r], C["cb"]], wr=[cvb[r]])
            prog.add("dve", lambda e, r=r, i=i: e.scalar_tensor_tensor(out=cv[r][:, :], in0=vbuf[:, i, 0:512], scalar=cw[:, i * 3:i * 3 + 1],
                                                                     in1=cv[r][:, :], op0=ALU.mult, op1=ALU.add),
                     rd=[vb[i], cvb[r], C["cb"]], wr=[cvb[r]])
            prog.add("dve", lambda e, r=r, i=i, bps=bps: e.tensor_tensor(out=zT[:, i, :], in0=bps[:, :], in1=cv[r][:, :], op=ALU.mult),
                     rd=[bpb, cvb[r]], wr=[zb[i]])
            prog.add("act", lambda e, i=i: e.copy(out=vbuf[:, i, 0:2], in_=vbuf[:, i, 512:514]), rd=[vb[i]], wr=[vb[i]])
        for i in range(8):
            w, wb = ws.fetch(24 + i)
            ps, pb = banks.next()
            for k in range(8):
                prog.mm(ps[:, :], w[:, k * 128:(k + 1) * 128], zT[:, k, :], k == 0, k == 7, rd=[wb, zb[k]], wr=[pb])
            prog.add("dve", lambda e, i=i, ps=ps, xt=xt: e.tensor_tensor(out=xt[:, i, :], in0=ps[:, :], in1=xt[:, i, :], op=ALU.add),
                     rd=[pb, xb[i]], wr=[xb[i]])
        rmsnorm(prog, banks, C, xt, xb, 8, hT, hb)
        mlp(prog, banks, C, ws, 32, 64, hT, hb, xt, xb)
        xv = xt.rearrange("p c (a w q) -> p c a w q", a=2, w=2)
        xmv = xm.rearrange("p c (a q) -> p c a q", a=2)
        prog.add("dve", lambda e, xv=xv: e.tensor_scalar(out=xmv, in0=xv[:, :, :, 0, :], scalar1=par[:, 0:1], scalar2=None, op0=ALU.mult),
                 rd=xb + [C["cb"]], wr=[xmb])
        prog.add("dve", lambda e, xv=xv: e.scalar_tensor_tensor(out=xmv, in0=xv[:, :, :, 1, :], scalar=par[:, 1:2], in1=xmv,
                                                               op0=ALU.mult, op1=ALU.add), rd=xb + [xmb, C["cb"]], wr=[xmb])
        prog.dma("pool", D["x1m"].rearrange("c p t -> p c t")[:, :, T * 256:(T + 1) * 256], xm[:, :, :], rd=[xmb], wr=[x1mb])
        rmsnorm(prog, banks, C, xt, xb, 16, hT, hb)
        for c in range(8):
            w, wb = ws.fetch(96 + c)
            ps, pb = banks.next()
            for k in range(8):
                prog.mm(ps[:, :], w[:, k * 128:(k + 1) * 128], hT[:, k, :], k == 0, k == 7, rd=[wb, hb[k]], wr=[pb])
            prog.add("act", lambda e, c=c, ps=ps: e.copy(out=kst[:, c, :], in_=ps[:, :]), rd=[pb], wr=[kstb[c]])
        prog.dma("pool", D["kT"].rearrange("c p t -> p c t")[:, :, T * 512:(T + 1) * 512], kst[:, 0:4, :], rd=kstb[0:4], wr=[kTb])
        prog.dma("pool", D["raw"].rearrange("c p t -> p c t")[:, :, T * 512:(T + 1) * 512], kst[:, 4:8, :], rd=kstb[4:8], wr=[rawb])
        wv = [ws.fetch(104 + j) for j in range(4)]
        for tt in range(4):
            ps, pb = banks.next()
            for k in range(8):
                w, wb = wv[k // 2]
                prog.mm(ps[:, :], hT[:, k, tt * 128:(tt + 1) * 128], w[:, (k % 2) * 512:(k % 2 + 1) * 512], k == 0, k == 7,
                        rd=[wb, hb[k]], wr=[pb])
            prog.add("act", lambda e, tt=tt, ps=ps: e.copy(out=vst[:, tt, :], in_=ps[:, :]), rd=[pb], wr=[vstb[tt]])
        prog.dma("pool", D["vsc"][:, T * 4:(T + 1) * 4, :], vst[:, :, :], rd=vstb, wr=[vscb])

    prog.barrier()
    prog.aoff = D["mark"]
    ph1z = [[prog.sbuf(f"ph1z_{i}_{h}", [128, 32, 256], BF16) for h in range(2)] for i in range(2)]
    ph1f = [prog.sbuf(f"ph1f{i}", [128, 2048], F32) for i in range(2)]
    ph1fb = [prog.buf() for _ in range(2)]
    ph1b = [prog.buf() for _ in range(2)]
    for kv in range(2):
        for h in range(2):
            prog.add("pool", lambda e, kv=kv, h=h: e.memset(ph1z[kv][h][:, :, :], 0.0), wr=[ph1b[kv]])
    for kv in range(2):
        for q4 in range(4):
            f_, fb_ = ph1f[q4 % 2], ph1fb[q4 % 2]
            prog.dma("sp", f_[:, :], D["ph1"][kv, :, q4 * 2048:(q4 + 1) * 2048], wr=[fb_])
            for h in range(2):
                eng = "dve" if h == 0 else "act"
                if h == 0:
                    prog.add("dve", lambda e, kv=kv, q4=q4, h=h, f_=f_: e.tensor_copy(
                        out=ph1z[kv][h].rearrange("p t j -> p (t j)")[h * 64:h * 64 + 64, q4 * 2048:(q4 + 1) * 2048], in_=f_[h * 64:h * 64 + 64, :]),
                        rd=[fb_, ph1b[kv]], wr=[ph1b[kv]])
                else:
                    prog.add("act", lambda e, kv=kv, q4=q4, h=h, f_=f_: e.copy(
                        out=ph1z[kv][h].rearrange("p t j -> p (t j)")[h * 64:h * 64 + 64, q4 * 2048:(q4 + 1) * 2048], in_=f_[h * 64:h * 64 + 64, :]),
                        rd=[fb_, ph1b[kv]], wr=[ph1b[kv]])
    sm32 = prog.sbuf("sm32", [128, 256 + 128 + 64], F32)
    smb = prog.buf()
    prog.dma("sp", sm32[:, 0:256], D["ph2k"], wr=[smb])
    prog.dma("sp", sm32[:, 256:384], D["ph2v"], wr=[smb])
    prog.dma("sp", sm32[:, 384:416], D["peT"][0], wr=[smb])
    prog.dma("sp", sm32[:, 416:448], D["peT"][1], wr=[smb])
    sm16 = prog.sbuf("sm16", [128, 448], BF16)
    sm16b = prog.buf()
    prog.add("dve", lambda e: e.tensor_copy(out=sm16[:, :], in_=sm32[:, :]), rd=[smb], wr=[sm16b])
    ph2k = sm16[:, 0:256].rearrange("p (k m) -> p k m", k=2)
    ph2v = sm16[:, 256:384].rearrange("p (k m) -> p k m", k=2)
    pe16 = sm16[:, 384:448].rearrange("p (v t) -> p v t", v=2)
    bpe = prog.sbuf("bpe", [128, 4], F32)
    bpeb = prog.buf()
    for kv in range(2):
        for mc in range(2):
            ps, pb = banks.next()
            for t in range(32):
                prog.mm(ps[:, 0:1], ph1z[kv][0][:, t, mc * 128:(mc + 1) * 128], pe16[:, kv, t:t + 1], t == 0, t == 31,
                        rd=[ph1b[kv], sm16b], wr=[pb])
            prog.add("dve", lambda e, kv=kv, mc=mc, ps=ps: e.tensor_copy(out=bpe[:, kv * 2 + mc:kv * 2 + mc + 1], in_=ps[:, 0:1]),
                     rd=[pb], wr=[bpeb])
    raws = [prog.sbuf(f"raws{i}", [128, 8192], BF16) for i in range(2)]
    rawsb = [prog.buf() for _ in range(2)]
    hidc = prog.sbuf("hidc", [128, 2, 512], BF16)
    hidcb = prog.buf()
    kcT = prog.sbuf("kcT", [128, 2, 512], BF16)
    kcTb = prog.buf()
    vc = prog.sbuf("vcA", [128, 4, 4, 68], BF16)
    vcb = prog.buf()
    prog.add("pool", lambda e: e.memset(hidc[:, :, :], 0.0), wr=[hidcb])
    prog.add("pool", lambda e: e.memset(kcT[:, :, :], 0.0), wr=[kcTb])
    prog.add("pool", lambda e: e.memset(vc[:, :, :, :], 1.0), wr=[vcb])
    NB = 511
    for c in range(4):
        kv, P = c // 2, c % 2
        rs, rsb = raws[c % 2], rawsb[c % 2]
        prog.dma("sp", rs[:, :], D["raw"][c], rd=[rawb], wr=[rsb])
        for half in range(2):
            g = 2 * P + half
            lo, hi = half * 64, half * 64 + 64
            for mc in range(2):
                ps, pb = banks.next()
                for t in range(32):
                    rhs = rs[:, t:t + 16 * (NB - 1) + 1:16]
                    prog.mm(ps[:, 0:NB], ph1z[kv][half][:, t, mc * 128:(mc + 1) * 128], rhs, t == 0, t == 31, rd=[ph1b[kv], rsb], wr=[pb])
                prog.add("act", lambda e, mc=mc, ps=ps, kv=kv: e.activation(out=hidc[:, mc, 0:NB], in_=ps[:, 0:NB], func=AF.Silu,
                                                                           bias=bpe[:, kv * 2 + mc:kv * 2 + mc + 1]),
                         rd=[pb, bpeb], wr=[hidcb])
            if kv == 0:
                ps, pb = banks.next()
                for mc in range(2):
                    prog.mm(ps[:, 0:NB], ph2k[:, mc, :], hidc[:, mc, 0:NB], mc == 0, mc == 1, rd=[sm16b, hidcb], wr=[pb])
                prog.add("dve", lambda e, ps=ps, lo=lo, hi=hi, P=P: e.tensor_copy(out=kcT[lo:hi, P, 0:NB], in_=ps[lo:hi, 0:NB]),
                         rd=[pb], wr=[kcTb])
            else:
                for nt in range(4):
                    ps, pb = banks.next()
                    for mc in range(2):
                        prog.mm(ps[:, 0:64], hidc[:, mc, nt * 128:(nt + 1) * 128], ph2v[:, mc, :], mc == 0, mc == 1,
                                rd=[sm16b, hidcb], wr=[pb])
                    prog.add("dve", lambda e, ps=ps, nt=nt, g=g: e.tensor_copy(out=vc[:, nt, g, 0:64], in_=ps[:, 0:64]),
                             rd=[pb], wr=[vcb])
    prog.dma("pool", D["kc"], kcT[:, :, :], rd=[kcTb], wr=[D["kc_b"]])
    prog.dma("pool", D["vc"], vc[:, :, :, :], rd=[vcb], wr=[D["vc_b"]])


NEGV = -30000.0
NSLAB_B1 = 8 + 1
NSLAB_B2 = 8 + 32 + 32


def t5_bucket_np(d):
    d = np.maximum(np.asarray(d, np.int64), 0)
    logd = np.log(np.maximum(d, 1).astype(np.float32) / np.float32(16))
    large = 16 + (logd / np.float32(math.log(8.0)) * np.float32(16)).astype(np.int32)
    large = np.minimum(large, 31)
    return np.where(d < 16, d, large).astype(np.int64)


def host_consts_B(parity):
    p = parity
    bf = ml_dtypes.bfloat16
    out = {}
    def onehot(dvals):
        L = len(dvals)
        oh = np.zeros((33, L), np.float32)
        for e, d in enumerate(dvals):
            if d >= 0:
                oh[t5_bucket_np(d), e] += 1.0
                oh[31, e] -= 1.0
            else:
                oh[32, e] = NEGV
        return oh
    out["OHn"] = onehot(np.arange(512) - 255 + 128 * p)
    out["OHc"] = onehot(np.arange(384) - 127)
    k = np.arange(128)[:, None]
    q = np.arange(128)[None, :]
    tw = []
    for off in (3, 4):
        d = q - k + 128 * (p + off)
        m = np.where(d < 512, 0.0, NEGV).astype(np.float32)
        tw.append(np.tile(m, (1, 4)))
    out["Tw"] = np.stack(tw, axis=1).astype(bf)
    W = 640
    A = np.zeros((33, W), np.float32)
    OFF = 496
    for r in range(16):
        A[r, OFF + 8 * p + 6 - r] = 1.0
    A[32, OFF + 8 * p + 7:] = 1.0
    Afull = np.zeros((128, W), np.float32)
    Afull[:33] = A
    out["Acmp"] = Afull.astype(bf)
    M = np.zeros((512, 128), np.float32)
    for j in range(128):
        for n, w in ((4 * j - 1, 1), (4 * j, 2), (4 * j + 1, 2), (4 * j + 2, 2), (4 * j + 3, 1)):
            if 0 <= n < 511:
                M[n, j] += w
    out["Mmat"] = np.ascontiguousarray(M.reshape(4, 128, 128).transpose(1, 0, 2)).reshape(128, 512).astype(bf)
    keep = np.ones((128, 256), np.float32)
    add = np.zeros((128, 256), np.float32)
    for qq in range(128):
        cbp = 2 * p + (1 if qq >= 64 else 0)
        for m in range(256):
            jp = m - 124
            if jp > cbp:
                keep[qq, m] = 0.0; add[qq, m] = -1e30
            elif jp == cbp:
                keep[qq, m] = 0.0; add[qq, m] = 2e9
            elif jp == cbp - 1:
                keep[qq, m] = 0.0; add[qq, m] = 1e9
    out["keepadd"] = np.concatenate([keep, add], axis=1)
    eye = np.eye(128, dtype=np.float32)
    out["identf"] = eye
    mats = np.concatenate([eye, eye[::-1], np.tile(eye, (1, 4))], axis=1)
    out["mats"] = mats.astype(bf)
    E = np.zeros((128, 64, 128), np.float32)
    for m in range(64):
        E[2 * m, m, 0:64] = 1.0
        E[2 * m + 1, m, 64:128] = 1.0
    out["E64"] = E.reshape(128, 8192).astype(bf)
    return out


def host_slabs_B(inp):
    def slab(W, cols):
        return np.ascontiguousarray(W[:, cols].reshape(8, 128, 128).transpose(1, 0, 2))
    wqg = inp["b_w_qg"][0]
    b1 = []
    for P in range(2):
        for hg in range(4):
            cols = np.concatenate([((2 * P) * 4 + hg) * 64 + np.arange(64), ((2 * P + 1) * 4 + hg) * 64 + np.arange(64)])
            b1.append(slab(wqg, cols).reshape(128, 1024))
    wg = wqg[:, 1024:1072].reshape(8, 128, 48).transpose(1, 0, 2).reshape(128, 384)
    gs = np.zeros((128, 1024), np.float32)
    gs[:, :384] = wg
    b1.append(gs)
    b2 = []
    wo = inp["b_w_o"][0]
    for i in range(8):
        b2.append(slab(wo, i * 128 + np.arange(128)).reshape(128, 1024))
    w1 = inp["mlp_w1"][1]
    for j in range(32):
        b2.append(slab(w1, j * 128 + np.arange(128)).reshape(128, 1024))
    w2 = inp["mlp_w2"][1]
    for i in range(8):
        for kq in range(4):
            b2.append(slab(w2[kq * 1024:(kq + 1) * 1024], i * 128 + np.arange(128)).reshape(128, 1024))
    return np.stack(b1).astype(np.float32), np.stack(b2).astype(np.float32)


def host_small_B(inp):
    def cvec(v):
        return v.reshape(8, 128).T
    g = np.stack([cvec(inp["norm_mix"][1]), cvec(inp["norm_mlp"][1]), cvec(inp["final_norm"])], axis=1)
    return dict(gB=np.ascontiguousarray(g.reshape(128, 24), dtype=np.float32),
                relb=np.ascontiguousarray(inp["rel_bias"], dtype=np.float32))


def cast_B(prog, D, part=None, nparts=1):
    if "wb1b" not in D:
        D["wb1b"] = [prog.buf(f"wbfB1_{s}") for s in range(NSLAB_B1)]
        D["wb2b"] = [prog.buf(f"wbfB2_{s}") for s in range(NSLAB_B2)]
    allslabs = [(1, s) for s in range(NSLAB_B1)] + [(2, s) for s in range(NSLAB_B2)]
    if part is not None:
        per = (len(allslabs) + nparts - 1) // nparts
        allslabs = allslabs[part * per:(part + 1) * per]
    s1 = [s for w, s in allslabs if w == 1]
    s2 = [s for w, s in allslabs if w == 2]
    if s1:
        cast_weights(prog, D["wslabB1"], D["wbfB1"], s1, D["wb1b"], "B1")
    if s2:
        cast_weights(prog, D["wslabB2"], D["wbfB2"], s2, D["wb2b"], "B2")
    return D["wb1b"], D["wb2b"]


def build_B(prog, nc, D, banks, C, NI=32):
    wb1, wb2 = D["wbfB1"], D["wbfB2"]
    if "wb1b" in D:
        wb1b, wb2b = D["wb1b"], D["wb2b"]
    else:
        wb1b, wb2b = cast_B(prog, D)
    prog.barrier()
    prog.aoff = D["mark"]

    cb = C["cb"]
    g32 = prog.sbuf("g32B", [128, 24], F32)
    prog.dma("sp", g32[:, :], D["gB"], wr=[cb])
    C["g32"] = g32
    mats = prog.sbuf("mats", [128, 768], BF16)
    prog.dma("sp", mats[:, :], D["mats"], wr=[cb])
    identb, Jm, I4 = mats[:, 0:128], mats[:, 128:256], mats[:, 256:768]
    identf = prog.sbuf("identf", [128, 128], F32)
    prog.dma("sp", identf[:, :], D["identf"], wr=[cb])
    Tw = prog.sbuf("Tw", [128, 2, 512], BF16)
    prog.dma("sp", Tw[:, :, :], D["Tw"], wr=[cb])
    Acmp = prog.sbuf("Acmp", [128, 640], BF16)
    prog.dma("sp", Acmp[:, :], D["Acmp"], wr=[cb])
    Mmat = prog.sbuf("Mmat", [128, 4, 128], BF16)
    prog.dma("sp", Mmat.rearrange("p a b -> p (a b)"), D["Mmat"], wr=[cb])
    keepadd = prog.sbuf("keepadd", [128, 512], F32)
    prog.dma("sp", keepadd[:, :], D["keepadd"], wr=[cb])
    E64 = prog.sbuf("E64", [128, 8192], BF16)
    prog.dma("sp", E64[:, :], D["E64"], wr=[cb])
    stop_at(23)
    Tn = prog.sbuf("Tn", [128, 12, 512], BF16)
    Bc = prog.sbuf("Bc", [128, 4, 512], BF16)
    mark2 = prog.aoff
    rbA = prog.sbuf("rbA", [128, 16], F32)
    ohn = prog.sbuf("ohn", [128, 512], F32)
    ohc = prog.sbuf("ohc", [128, 384], F32)
    tb = prog.buf("tblbuild")
    prog.add("dve", lambda e: e.memset(rbA[32:33, :], 1.0), wr=[tb])
    prog.dma("sp", rbA[0:32, :], D["relb"], rd=[tb], wr=[tb])
    prog.dma("sp", ohn[0:33, :], D["OHn"], wr=[tb])
    prog.dma("sp", ohc[0:33, :], D["OHc"], wr=[tb])
    vsb = prog.sbuf("vsb", [16, 896], F32)
    vsbb = prog.buf()
    ps, pb = banks.next()
    prog.mm(ps[0:16, 0:512], rbA[0:33, :], ohn[0:33, :], True, True, rd=[tb], wr=[pb])
    prog.add("dve", lambda e, ps=ps: e.tensor_copy(out=vsb[:, 0:512], in_=ps[0:16, 0:512]), rd=[pb], wr=[vsbb])
    ps, pb = banks.next()
    prog.mm(ps[0:16, 0:384], rbA[0:33, :], ohc[0:33, :], True, True, rd=[tb], wr=[pb])
    prog.add("dve", lambda e, ps=ps: e.tensor_copy(out=vsb[:, 512:896], in_=ps[0:16, 0:384]), rd=[pb], wr=[vsbb])
    vdb = prog.buf("vd")
    prog.dma("pool", D["Vd"], vsb[:, :], rd=[vsbb], wr=[vdb])
    tnb = prog.buf("Tn")
    prog.add("dve", lambda e: e.memset(Bc[:, :, :], 0.0), wr=[tnb])
    prog.add("dve", lambda e: e.memset(Bc[32:33, :, :], NEGV), rd=[tnb], wr=[tnb])
    stg = [prog.sbuf(f"tstg{i}", [128, 512], F32) for i in range(2)]
    stgb = [prog.buf() for _ in range(2)]
    Vd_t = D["Vd"].tensor
    n = 0
    for dl in range(3):
        for g in range(4):
            s, sb_ = stg[n % 2], stgb[n % 2]
            n += 1
            src = bass.AP(tensor=Vd_t, offset=(4 * g) * 896 + 128 * dl, ap=[[1, 128], [896, 4], [1, 128]])
            prog.dma("sp", s.rearrange("p (h q) -> p h q", h=4), src, rd=[vdb], wr=[sb_])
            prog.add("dve", lambda e, s=s, dl=dl, g=g: e.tensor_copy(out=Tn[:, dl * 4 + g, :], in_=s[:, :]), rd=[sb_], wr=[tnb])
    for g in range(4):
        s, sb_ = stg[n % 2], stgb[n % 2]
        n += 1
        src = bass.AP(tensor=Vd_t, offset=(4 * g) * 896 + 512, ap=[[16, 16], [896, 4], [1, 128]])
        prog.dma("sp", s[0:16, :].rearrange("p (h q) -> p h q", h=4), src, rd=[vdb], wr=[sb_])
        prog.add("dve", lambda e, s=s, g=g: e.tensor_copy(out=Bc[0:16, g, :], in_=s[0:16, :]), rd=[sb_], wr=[tnb])

    prog.barrier()
    stop_at(1)
    prog.aoff = mark2
    kcT = prog.sbuf("kcTB", [128, 2, 512], BF16)
    vc = prog.sbuf("vcB", [128, 4, 4, 68], BF16)
    kcb = prog.buf()
    prog.dma("sp", kcT[:, :, :], D["kc"], rd=[D["kc_b"]], wr=[kcb])
    prog.dma("sp", vc[:, :, :, :], D["vc"], rd=[D["vc_b"]], wr=[kcb])

    ksT = prog.sbuf("ksT", [128, 8192], BF16)
    kwT = prog.sbuf("kwT", [128, 8192], BF16)
    vs = prog.sbuf("vs", [128, 64, 2, 68], BF16)
    vw = prog.sbuf("vw", [128, 64, 2, 68], BF16)
    ksb, kwb = prog.buf("ksT"), prog.buf("kwT")
    vchain = prog.buf("vchain")
    vsb_ = [[prog.buf() for _ in range(2)] for _ in range(2)]
    vwb = [[prog.buf() for _ in range(2)] for _ in range(2)]
    prog.add("pool", lambda e: e.memset(vs[:, :, :, :], 1.0), wr=[x for r in vsb_ for x in r])
    prog.add("pool", lambda e: e.memset(vw[:, :, :, :], 1.0), wr=[x for r in vwb for x in r])

    wq = [prog.sbuf(f"wq{i}", [128, 1024], BF16) for i in range(4)]
    wgt = prog.sbuf("wgt", [128, 1024], BF16)
    wqb = prog.buf("wq")

    xq = [prog.sbuf(f"xq{i}", [128, 8, 128], F32) for i in range(2)]
    xqb = [[prog.buf() for _ in range(8)] for _ in range(2)]
    hq = prog.sbuf("hq", [128, 8, 128], BF16)
    hqb = [prog.buf() for _ in range(8)]
    NSLOT = 4
    qTz = [[prog.sbuf(f"qTz{s}_{i}", [128, 4, 128], BF16) for i in range(2)] for s in range(NSLOT)]
    qTb = [prog.buf() for _ in range(NSLOT)]
    for s_ in range(NSLOT):
        for i_ in range(2):
            prog.add("pool", lambda e, s_=s_, i_=i_: e.memset(qTz[s_][i_][:, :, :], 0.0), rd=[qTb[s_]], wr=[qTb[s_]])
    gts = [prog.sbuf(f"gts{s}", [128, 48], F32) for s in range(NSLOT)]
    gtsb = [prog.buf() for _ in range(NSLOT)]
    NPT = 4
    pts = [prog.sbuf(f"pt{i}", [128, 512], BF16) for i in range(NPT)]
    ptb = [prog.buf() for _ in range(NPT)]
    pcs = [prog.sbuf(f"pc{i}", [128, 512], BF16) for i in range(4)]
    pcb = [prog.buf() for _ in range(4)]
    accs = [[prog.sbuf(f"accs{s}_{i}", [128, 512], BF16) for i in range(3)] for s in range(2)]
    accsb = [[prog.buf() for _ in range(3)] for _ in range(2)]
    for s_ in range(2):
        for i_ in range(3):
            prog.add("pool", lambda e, s_=s_, i_=i_: e.memset(accs[s_][i_][:, :], 0.0), wr=[accsb[s_][i_]])
    obr = [[prog.sbuf(f"obr{s}_{i}", [128, 4, 66], F32) for i in range(3)] for s in range(2)]
    obrb = [[prog.buf() for _ in range(3)] for _ in range(2)]
    rden = [prog.sbuf(f"rden{s}", [128, 12], F32) for s in range(2)]
    rdenb = [[prog.buf() for _ in range(3)] for _ in range(2)]
    fbr = prog.sbuf("fbr", [128, 12], F32)
    fbrb = prog.buf()
    simp = prog.sbuf("simp", [128, 128], F32)
    score = prog.sbuf("score", [128, 128], F32)
    score2 = prog.sbuf("score2", [128, 128], F32)
    scb = prog.buf()
    mx = prog.sbuf("mx", [128, 24], F32)
    negm = prog.sbuf("negm", [128, 128], BF16)
    negmb = prog.buf()
    negmT4 = prog.sbuf("negmT4", [128, 512], BF16)
    negmTb = prog.buf()
    tmpo = prog.sbuf("tmpo", [128, 4, 64], F32)
    tmpob = prog.buf()
    otoks = [prog.sbuf(f"otok{i}", [128, 512], BF16) for i in range(2)]
    otokbs = [prog.buf() for _ in range(2)]
    oTst = prog.sbuf("oTst", [128, 4, 128], BF16)
    oTstb = prog.buf()

    misc = (banks.t[3], banks.b[3])
    acc_c = (banks.t[4], banks.b[4])
    acc_s = (banks.t[5], banks.b[5])
    acc_w = [(banks.t[6], banks.b[6]), (banks.t[7], banks.b[7])]
    rot = [0]

    def sbank():
        k = rot[0] % 3
        rot[0] += 1
        return banks.t[k], banks.b[k]

    x1mv = D["x1m"].rearrange("c p t -> p c t")
    oTv = D["oT"].rearrange("c p t -> p c t")
    kTd, vscd = D["kT"], D["vsc"]
    oTb = D["oT_b"]
    npt = [0]

    pend = []
    DEPTH = 2

    def flush_one():
        idx, n_, buf, bufb, vl, vrd, acc, accb = pend.pop(0)
        prog.mm(acc[0:65, :], vl, buf[:, :], idx == 0, idx == n_ - 1, rd=vrd + [bufb], wr=[accb])

    def flush_all():
        while pend:
            flush_one()

    def emit_branch(tiles, acc, accb, drain=False, off=0, total=None):
        n = total if total is not None else len(tiles)
        for idx0, (mms, vl, vrd, fixed) in enumerate(tiles):
            idx = idx0 + off
            ps, pb = sbank()
            for j, (l, r, rd) in enumerate(mms):
                prog.mm(ps[:, :], l, r, j == 0, j == len(mms) - 1, rd=rd, wr=[pb])
            if fixed is not None:
                buf, bufb = fixed
            else:
                s = npt[0] % NPT
                npt[0] += 1
                buf, bufb = pts[s], ptb[s]
            prog.add("act", lambda e, ps=ps, buf=buf: e.activation(out=buf[:, :], in_=ps[:, :], func=AF.Exp), rd=[pb], wr=[bufb])
            pend.append((idx, n, buf, bufb, vl, vrd, acc, accb))
            if len(pend) > DEPTH:
                flush_one()
        if drain:
            flush_all()

    def post(br, s, acc, accb, stages=(1, 2, 3), bank=None):
        post_branch(prog, br, acc, accb, accs[s], accsb[s], obr[s], obrb[s], rden[s], rdenb[s], identb, cb, bank or misc, stages)

    seqpos = {}

    def prepA(i):
        xt, xb = xq[seqpos[i] % 2], xqb[seqpos[i] % 2]
        prog.dma("sp", xt[:, :, :], x1mv[:, :, i * 128:(i + 1) * 128], rd=[D["x1m_b"]], wr=xb)
        rmsnorm_b1(prog, misc, C, xt, xb, g32, hq, hqb)

    def prepB(i):
        si = seqpos[i] % NSLOT
        ps, pb = misc
        for hg in range(4):
            for k in range(8):
                prog.mm(ps[:, hg * 128:(hg + 1) * 128], wq[hg][:, k * 128:(k + 1) * 128], hq[:, k, :], k == 0, k == 7,
                        rd=[wqb, hqb[k]], wr=[pb], skip_group_check=True)
        for gq in range(2):
            prog.add("dve", lambda e, ps=ps, gq=gq, si=si: e.tensor_scalar(out=qTz[si][gq].rearrange("p a b -> p (a b)")[gq * 64:gq * 64 + 64, :],
                                                                          in0=ps[gq * 64:gq * 64 + 64, :], scalar1=0.125, scalar2=None, op0=ALU.mult),
                     rd=[pb, qTb[si]], wr=[qTb[si]])
        ps, pb = misc
        for k in range(8):
            prog.mm(ps[:, 0:48], hq[:, k, :], wgt[:, k * 48:(k + 1) * 48], k == 0, k == 7, rd=[wqb, hqb[k]], wr=[pb])
        prog.add("act", lambda e, ps=ps, si=si: e.activation(out=gts[si][:, :], in_=ps[:, 0:48], func=AF.Exp, scale=-1.0), rd=[pb], wr=[gtsb[si]])
        prog.add("dve", lambda e, si=si: e.tensor_scalar(out=gts[si][:, :], in0=gts[si][:, :], scalar1=1.0, scalar2=None, op0=ALU.add),
                 rd=[gtsb[si]], wr=[gtsb[si]])
        prog.add("dve", lambda e, si=si: e.reciprocal(out=gts[si][:, :], in_=gts[si][:, :]), rd=[gtsb[si]], wr=[gtsb[si]])

    def head(n, P, i, gp):
        s = n % 2
        g = 2 * P + gp
        qg = qTz[seqpos[i] % NSLOT][gp][:, :, :]
        qb_ = qTb[seqpos[i] % NSLOT]
        ntc = (2 * i + 1) // 16 + 1
        cmp_tiles = []
        for nt in range(ntc):
            a0 = 128 * nt - 16 * i + 496
            cmp_tiles.append(([(kcT[:, P, nt * 128:(nt + 1) * 128], qg, [kcb, qb_]),
                               (Acmp[:, a0:a0 + 128], Bc[:, g, :], [cb, tnb])],
                              vc[:, nt, g, 0:65], [kcb], (pcs[nt], pcb[nt])))
        win_tiles = []
        for kt in range(2 * i - 4, 2 * i + 2):
            if kt < 0:
                continue
            dl = 2 * i + 1 - kt
            mms = [(kwT[:, kt * 128:(kt + 1) * 128], qg, [kwb, qb_])]
            if dl <= 2:
                mms.append((Jm, Tn[:, dl * 4 + g, :], [cb, tnb]))
            elif dl >= 4:
                mms.append((identb, Tw[:, dl - 4, :], [cb]))
            win_tiles.append((mms, vw[:, kt, gp, 0:65], [vwb[kt // 32][gp]], None))
        emit_branch(cmp_tiles, *acc_c, drain=True)
        nw0 = 0
        emit_branch(win_tiles[:nw0], *acc_w[s], off=0, total=len(win_tiles))
        post(0, s, *acc_c)
        ps, pb = misc
        for hg in range(4):
            for nt in range(ntc):
                prog.mm(ps[:, hg * 128:(hg + 1) * 128], pcs[nt][:, hg * 128:(hg + 1) * 128], Mmat[:, nt, :], nt == 0, nt == ntc - 1,
                        rd=[pcb[nt], cb], wr=[pb], skip_group_check=True)
        rd_ = rden[s]
        prog.add("dve", lambda e, ps=ps: e.tensor_scalar(out=simp[:, :], in0=ps[:, 0:128], scalar1=rd_[:, 0:1], scalar2=None, op0=ALU.mult),
                 rd=[pb, rdenb[s][0]], wr=[scb])
        for hg in range(1, 4):
            prog.add("dve", lambda e, ps=ps, hg=hg: e.scalar_tensor_tensor(out=simp[:, :], in0=ps[:, hg * 128:(hg + 1) * 128],
                                                                         scalar=rd_[:, hg:hg + 1], in1=simp[:, :], op0=ALU.mult, op1=ALU.add),
                     rd=[pb, rdenb[s][0], scb], wr=[scb])
        m0 = 124 - 4 * i
        prog.add("dve", lambda e: e.tensor_tensor(out=score[:, :], in0=simp[:, :], in1=keepadd[:, m0:m0 + 128], op=ALU.mult), rd=[scb, cb], wr=[scb])
        prog.add("dve", lambda e: e.tensor_tensor(out=score[:, :], in0=score[:, :], in1=keepadd[:, 256 + m0:256 + m0 + 128], op=ALU.add),
                 rd=[scb, cb], wr=[scb])
        prog.add("dve", lambda e: e.memset(score[:, 0:1], 3e9), rd=[scb], wr=[scb])
        prog.add("dve", lambda e: e.max(out=mx[:, 0:8], in_=score[:, :]), rd=[scb], wr=[scb])
        prog.add("dve", lambda e: e.match_replace(out=score2[:, :], in_to_replace=mx[:, 0:8], in_values=score[:, :], imm_value=-3e38), rd=[scb], wr=[scb])
        prog.add("dve", lambda e: e.max(out=mx[:, 8:16], in_=score2[:, :]), rd=[scb], wr=[scb])
        prog.add("dve", lambda e: e.tensor_reduce(out=mx[:, 16:17], in_=mx[:, 8:16], axis=mybir.AxisListType.X, op=ALU.min), rd=[scb], wr=[scb])
        prog.add("dve", lambda e: e.tensor_scalar(out=mx[:, 16:17], in0=mx[:, 16:17], scalar1=-1e29, scalar2=None, op0=ALU.max), rd=[scb], wr=[scb])
        prog.add("dve", lambda e: e.tensor_scalar(out=negm[:, :], in0=score[:, :], scalar1=mx[:, 16:17], scalar2=NEGV, op0=ALU.is_lt, op1=ALU.mult),
                 rd=[scb], wr=[negmb])
        emit_branch(win_tiles[nw0:], *acc_w[s], off=nw0, total=len(win_tiles))
        ps, pb = misc
        prog.mm(ps[:, :], negm[:, :], I4, True, True, rd=[negmb, cb], wr=[pb])
        prog.add("dve", lambda e, ps=ps: e.tensor_copy(out=negmT4[:, :], in_=ps[:, :]), rd=[pb], wr=[negmTb])

    def sel(n, P, i, gp):
        g = 2 * P + gp
        qg = qTz[seqpos[i] % NSLOT][gp][:, :, :]
        qb_ = qTb[seqpos[i] % NSLOT]
        sel_tiles = []
        for kt in range(2 * i + 2):
            dl = 2 * i + 1 - kt
            mms = [(ksT[:, kt * 128:(kt + 1) * 128], qg, [ksb, qb_])]
            if dl <= 2:
                mms.append((Jm, Tn[:, dl * 4 + g, :], [cb, tnb]))
            mms.append((E64[:, kt * 128:(kt + 1) * 128], negmT4[:, :], [negmTb, cb]))
            sel_tiles.append((mms, vs[:, kt, gp, 0:65], [vsb_[kt // 32][gp]], None))
        emit_branch(sel_tiles, *acc_s)

    def tail(n, P, i, gp):
        s = n % 2
        g = 2 * P + gp
        bk2 = sbank()
        post(2, s, *acc_w[s], stages=(1,))
        post(1, s, *acc_s, stages=(1,))
        post(2, s, *acc_w[s], stages=(2,))
        post(1, s, *acc_s, stages=(2,), bank=bk2)
        post(2, s, *acc_w[s], stages=(3,))
        post(1, s, *acc_s, stages=(3,), bank=bk2)
        gt, gtb = gts[seqpos[i] % NSLOT], gtsb[seqpos[i] % NSLOT]
        otok, otokb = otoks[seqpos[i] % 2], otokbs[seqpos[i] % 2]
        ob, obb, rd_ = obr[s], obrb[s], rden[s]
        for br in range(3):
            gsl = gt[:, g * 12 + br:g * 12 + br + 10:3]
            prog.add("dve", lambda e, br=br, gsl=gsl: e.tensor_tensor(out=fbr[:, br * 4:(br + 1) * 4], in0=gsl, in1=rd_[:, br * 4:(br + 1) * 4], op=ALU.mult),
                     rd=[gtb, rdenb[s][br]], wr=[fbrb])
        for hg in range(4):
            prog.add("dve", lambda e, hg=hg: e.tensor_scalar(out=tmpo[:, hg, :], in0=ob[0][:, hg, 0:64], scalar1=fbr[:, hg:hg + 1], scalar2=None, op0=ALU.mult),
                     rd=[obb[0], fbrb], wr=[tmpob])
            prog.add("dve", lambda e, hg=hg: e.scalar_tensor_tensor(out=tmpo[:, hg, :], in0=ob[1][:, hg, 0:64], scalar=fbr[:, 4 + hg:5 + hg],
                                                                    in1=tmpo[:, hg, :], op0=ALU.mult, op1=ALU.add), rd=[obb[1], fbrb, tmpob], wr=[tmpob])
            c0 = gp * 256 + hg * 64
            prog.add("dve", lambda e, hg=hg, c0=c0: e.scalar_tensor_tensor(out=otok[:, c0:c0 + 64], in0=ob[2][:, hg, 0:64], scalar=fbr[:, 8 + hg:9 + hg],
                                                                           in1=tmpo[:, hg, :], op0=ALU.mult, op1=ALU.add),
                     rd=[obb[2], fbrb, tmpob], wr=[otokb])
        if gp == 1:
            def tail_b():
                ps, pb = misc
                for c in range(4):
                    prog.mm(ps[:, c * 128:(c + 1) * 128], otok[:, c * 128:(c + 1) * 128], identb, True, True, rd=[otokb, cb], wr=[pb], skip_group_check=True)
                prog.add("dve", lambda e, ps=ps: e.tensor_copy(out=oTst.rearrange("p a b -> p (a b)"), in_=ps[:, :]), rd=[pb], wr=[oTstb])
                prog.dma("pool", oTv[:, P * 4:(P + 1) * 4, i * 128:(i + 1) * 128], oTst[:, :, :], rd=[oTstb], wr=[oTb])
            deferred.append(tail_b)

    nseq = 0
    deferred = []
    for P in range(2):
        for hg in range(4):
            prog.dma("sp", wq[hg][:, :], wb1[P * 4 + hg], rd=[wb1b[P * 4 + hg]], wr=[wqb])
        prog.dma("sp", wgt[:, :], wb1[8], rd=[wb1b[8]], wr=[wqb])
        prog.dma("sp", kwT[:, :], kTd[2 + P], rd=[D["kT_b"]], wr=[kwb])
        for ty, vt, vb_ in ((1, vw, vwb), (0, vs, vsb_)):
            for k2 in range(2):
                for g_ in range(2):
                    c0 = ty * 256 + P * 128 + g_ * 64
                    prog.dma("sp", vt[:, k2 * 32:(k2 + 1) * 32, g_, 0:64], vscd[:, k2 * 32:(k2 + 1) * 32, c0:c0 + 64], rd=[D["vsc_b"], vchain],
                             wr=[vb_[k2][g_], vchain])
            if ty == 1:
                prog.dma("sp", ksT[:, :], kTd[P], rd=[D["kT_b"]], wr=[ksb])
        order = []
        lo_, hi_ = 0, NI - 1
        while lo_ <= hi_:
            order.append(lo_)
            if hi_ != lo_:
                order.append(hi_)
            lo_ += 1
            hi_ -= 1
        seqpos.clear()
        seqpos.update({i: k for k, i in enumerate(order)})
        pairs = [order[k:k + 2] for k in range(0, NI, 2)]
        groups = []
        for pr in pairs:
            for gp in range(2):
                for i in pr:
                    groups.append((i, gp))
        prep_at = {}
        for k, pr in enumerate(pairs[1:]):
            base = 4 * k if len(pairs[k]) == 2 else None
            prep_at[4 * k + 1] = pr[0]
            if len(pr) > 1:
                prep_at[4 * k + 2] = pr[1]
        for i in pairs[0]:
            prepA(i)
            prepB(i)
        head(nseq, P, groups[0][0], groups[0][1])
        for idx, (i, gp) in enumerate(groups):
            n = nseq + idx
            sel(n, P, i, gp)
            while deferred:
                deferred.pop(0)()
            x = prep_at.get(idx)
            if idx + 1 < len(groups):
                i2, gp2 = groups[idx + 1]
                if x is not None:
                    prepA(x)
                head(n + 1, P, i2, gp2)
                if x is not None:
                    prepB(x)
            else:
                flush_all()
            tail(n, P, i, gp)
        while deferred:
            deferred.pop(0)()
        flush_all()
        nseq += len(groups)
    stop_at(7)
    prog.barrier()
    prog.aoff = D["mark"]
    g32 = prog.sbuf("g32B2", [128, 24], F32)
    prog.dma("sp", g32[:, :], D["gB"], wr=[cb])
    C["g32"] = g32
    C["hid"] = prog.sbuf("hidB", [128, 32, 512], BF16)
    ws = WStream(prog, "wsB", 12, wb2, wb2b)
    xts = [prog.sbuf(f"xtB{i}", [128, 8, 512], F32) for i in range(2)]
    xbs = [[prog.buf() for _ in range(8)] for _ in range(2)]
    oTs = [prog.sbuf(f"oTs{i}", [128, 8, 512], BF16) for i in range(2)]
    oTsb = [prog.buf() for _ in range(2)]
    hT = prog.sbuf("hTB", [128, 8, 512], BF16)
    hb = [prog.buf() for _ in range(8)]
    yo = prog.sbuf("yo", [128, 8, 512], F32)
    yob = [prog.buf() for _ in range(8)]
    outv = D["outT"].rearrange("c p t -> p c t")
    for T in range(NI // 4):
        xt, xb = xts[T % 2], xbs[T % 2]
        oT_, oTb_ = oTs[T % 2], oTsb[T % 2]
        prog.dma("sp", xt[:, :, :], x1mv[:, :, T * 512:(T + 1) * 512], rd=[D["x1m_b"]], wr=xb)
        prog.dma("sp", oT_[:, :, :], oTv[:, :, T * 512:(T + 1) * 512], rd=[oTb], wr=[oTb_])
        for i in range(8):
            w, wb = ws.fetch(i)
            ps, pb = banks.next()
            for k in range(8):
                prog.mm(ps[:, :], w[:, k * 128:(k + 1) * 128], oT_[:, k, :], k == 0, k == 7, rd=[wb, oTb_], wr=[pb])
            prog.add("dve", lambda e, i=i, ps=ps, xt=xt: e.tensor_tensor(out=xt[:, i, :], in0=ps[:, :], in1=xt[:, i, :], op=ALU.add),
                     rd=[pb, xb[i]], wr=[xb[i]])
        rmsnorm(prog, banks, C, xt, xb, 8, hT, hb)
        mlp(prog, banks, C, ws, 8, 40, hT, hb, xt, xb)
        ps, pb = banks.next()
        for c in range(8):
            prog.add("act", lambda e, c=c, xt=xt: e.activation(out=C["sq"][:, c, :], in_=xt[:, c, :], func=AF.Square), rd=[xb[c]], wr=[C["sqb"][c]])
        for c in range(8):
            prog.mm(ps[:, :], C["ones"][:, :], C["sq"][:, c, :], c == 0, c == 7, rd=[C["sqb"][c], cb], wr=[pb])
        prog.add("act", lambda e, ps=ps: e.activation(out=C["rstd"][:, :], in_=ps[:, :], func=AF.Sqrt, bias=C["epsc"][:, 0:1], scale=1.0 / 1024.0),
                 rd=[pb, cb], wr=[C["rstdb"]])
        prog.add("dve", lambda e: e.reciprocal(out=C["rstd"][:, :], in_=C["rstd"][:, :]), rd=[C["rstdb"]], wr=[C["rstdb"]])
        for c in range(8):
            prog.add("dve", lambda e, c=c, xt=xt: e.scalar_tensor_tensor(out=yo[:, c, :], in0=xt[:, c, :], scalar=g32[:, 16 + c:17 + c], in1=C["rstd"][:, :],
                                                                        op0=ALU.mult, op1=ALU.mult), rd=[xb[c], C["rstdb"], cb], wr=[yob[c]])
        prog.dma("pool", outv[:, :, T * 512:(T + 1) * 512], yo[:, :, :], rd=yob, wr=[D["out_b"]])


class StopBuild(Exception):
    pass


STOP = int(os.environ.get("B_STOP", "0"))


GPSEL = int(os.environ.get("B_GP", "0"))


def stop_at(n, gp=None):
    if STOP == n and (gp is None or gp == GPSEL):
        raise StopBuild()


def rmsnorm_b1(prog, misc, C, xt, xb, g32, hT, hb):
    ps, pb = misc
    sq = C["sq"]
    prog.add("dve", lambda e: e.tensor_tensor(out=sq[:, :, 0:128], in0=xt[:, :, :], in1=xt[:, :, :], op=ALU.mult), rd=xb, wr=C["sqb"])
    for c in range(8):
        prog.mm(ps[:, 0:128], C["ones"][:, :], sq[:, c, 0:128], c == 0, c == 7, rd=[C["sqb"][c], C["cb"]], wr=[pb])
    rstd = C["rstd"]
    prog.add("act", lambda e: e.activation(out=rstd[:, 0:128], in_=ps[:, 0:128], func=AF.Ln, bias=C["epsc"][:, 0:1], scale=1.0 / 1024.0),
             rd=[pb, C["cb"]], wr=[C["rstdb"]])
    prog.add("act", lambda e: e.activation(out=rstd[:, 0:128], in_=rstd[:, 0:128], func=AF.Exp, scale=-0.5), rd=[C["rstdb"]], wr=[C["rstdb"]])
    for c in range(8):
        prog.add("dve", lambda e, c=c: e.scalar_tensor_tensor(out=hT[:, c, :], in0=xt[:, c, :], scalar=g32[:, c:c + 1], in1=rstd[:, 0:128],
                                                             op0=ALU.mult, op1=ALU.mult), rd=[xb[c], C["rstdb"], C["cb"]], wr=[hb[c]])


class banks_misc:
    def __init__(self, banks):
        self.banks = banks

    def next(self):
        return self.banks.t[7], self.banks.b[7]


def post_branch(prog, br, acc, accb, accs, accsb, obr, obrb, rden, rdenb, identb, cb, misc, stages=(1, 2, 3)):
    ps, pb = misc
    if 1 in stages:
        prog.add("dve", lambda e: e.tensor_copy(out=accs[br][0:65, :], in_=acc[0:65, :]), rd=[accb], wr=[accsb[br]])
    if 2 in stages:
        for hg in range(4):
            prog.mm(ps[:, hg * 66:hg * 66 + 65], accs[br][:, hg * 128:(hg + 1) * 128], identb[:, 0:65], True, True,
                    rd=[accsb[br], cb], wr=[pb], skip_group_check=True)
    if 3 in stages:
        prog.add("dve", lambda e, ps=ps: e.tensor_copy(out=obr[br].rearrange("p a b -> p (a b)")[:, 0:263], in_=ps[:, 0:263]), rd=[pb], wr=[obrb[br]])
        prog.add("dve", lambda e: e.tensor_scalar(out=rden[:, br * 4:(br + 1) * 4], in0=obr[br][:, :, 64], scalar1=1e-30, scalar2=None, op0=ALU.add),
                 rd=[obrb[br]], wr=[rdenb[br]])
        prog.add("dve", lambda e: e.reciprocal(out=rden[:, br * 4:(br + 1) * 4], in_=rden[:, br * 4:(br + 1) * 4]), rd=[rdenb[br]], wr=[rdenb[br]])

import time as _time
from concourse.bass_utils import run_bass_kernel_spmd

FUSED = True
NCORES = 8


def _declare(nc, prog, mode):
    D = {}

    def din(name, shape, dt=F32):
        D[name] = nc.dram_tensor(name, list(shape), dt, kind="ExternalInput").ap()

    def dten(name, shape, dt, kind):
        D[name] = nc.dram_tensor(name, list(shape), dt, kind=kind).ap()

    a_out_kind = {"A": "ExternalOutput", "AB": os.environ.get("AB_KIND", "Internal")}
    if mode in ("A", "AB"):
        din("xT", [1024, 8192]); din("wslabA", [NSLAB_A, 128, 1024]); din("gA", [128, 24]); din("cwA", [128, 24]); din("par", [128, 2])
        din("ph1", [2, 128, 8192]); din("ph2k", [128, 256]); din("ph2v", [128, 128]); din("peT", [2, 128, 32])
        dten("wbfA", [NSLAB_A, 128, 1024], BF16, "Internal")
        dten("raw", [4, 128, 8192], BF16, "Internal")
        k = a_out_kind[mode]
        dten("x1m", [8, 128, 4096], F32, k); dten("kT", [4, 128, 8192], BF16, k); dten("vsc", [128, 64, 512], BF16, k)
        dten("kc", [128, 2, 512], BF16, k); dten("vc", [128, 4, 4, 68], BF16, k)
    if mode == "B":
        din("x1m", [8, 128, 4096]); din("kT", [4, 128, 8192], BF16); din("vsc", [128, 64, 512], BF16)
        din("kc", [128, 2, 512], BF16); din("vc", [128, 4, 4, 68], BF16)
    if mode in ("B", "AB"):
        din("wslabB1", [NSLAB_B1, 128, 1024]); din("wslabB2", [NSLAB_B2, 128, 1024])
        din("gB", [128, 24]); din("relb", [32, 16]); din("OHn", [33, 512]); din("OHc", [33, 384])
        din("Tw", [128, 2, 512], BF16); din("Acmp", [128, 640], BF16); din("Mmat", [128, 512], BF16)
        din("keepadd", [128, 512]); din("identf", [128, 128]); din("mats", [128, 768], BF16); din("E64", [128, 8192], BF16)
        dten("wbfB1", [NSLAB_B1, 128, 1024], BF16, "Internal"); dten("wbfB2", [NSLAB_B2, 128, 1024], BF16, "Internal")
        dten("Vd", [16, 896], F32, "Internal"); dten("oT", [8, 128, 4096], BF16, "Internal")
        dten("outT", [8, 128, 4096], F32, "ExternalOutput")
    for n in ("x1m", "kT", "raw", "vsc", "kc", "vc", "oT", "out"):
        D[n + "_b"] = prog.buf(n, multi=True)
    return D


def _build(mode, NI=32):
    nc = bass.Bass("TRN2", target_bir_lowering=False)
    prog = Prog(nc)
    D = _declare(nc, prog, mode)
    prog.init_arena(206 * 1024)
    banks = Banks(prog)
    C = common_consts(prog)
    D["mark"] = prog.aoff
    outs = []
    if mode == "AB":
        def _precast(T):
            if 1 <= T <= 14:
                cast_B(prog, D, part=T - 1, nparts=14)
        D["precast"] = _precast
    if mode in ("A", "AB"):
        build_A(prog, nc, D, banks, C)
        outs += ["x1m", "kT", "vsc", "kc", "vc"]
    if mode == "AB":
        prog.barrier()
        prog.aoff = D["mark"]
    if mode in ("B", "AB") and not os.environ.get("SKIP_B"):
        try:
            build_B(prog, nc, D, banks, C, NI=NI)
        except StopBuild:
            prog.barrier()
        outs = ["out"] if mode == "AB" else ["out"]
    if os.environ.get("SKIP_B") or os.environ.get("B_STOP"):
        outs = ["x1m", "kT", "vsc", "kc", "vc"]
    prog.add("sp", None, rd=[D[n + "_b"] for n in outs])
    prog.emit()
    return nc, prog


def _maps_A(inp):
    slabs = host_slabs_A(inp)
    maps = []
    for c in range(NCORES):
        b, p = c // 2, c % 2
        m = dict(xT=np.ascontiguousarray(inp["x"][b].T), wslabA=slabs)
        m.update(host_small_A(inp, p))
        maps.append(m)
    return maps


def _maps_B(inp):
    b1, b2 = host_slabs_B(inp)
    small = host_small_B(inp)
    consts = [host_consts_B(0), host_consts_B(1)]
    maps = []
    for c in range(NCORES):
        m = dict(wslabB1=b1, wslabB2=b2)
        m.update(small)
        m.update(consts[c % 2])
        maps.append(m)
    return maps


def _assemble(results):
    out = np.empty((4, 8192, 1024), np.float32)
    for c in range(NCORES):
        b, p = c // 2, c % 2
        o = np.asarray(results[c]["outT"])
        tok = o.transpose(2, 0, 1).reshape(32, 128, 1024)
        out[b].reshape(32, 2, 128, 1024)[:, p] = tok
    return out


def kernel(**inputs):
    inp = {k: np.asarray(v) for k, v in inputs.items()}
    ids = list(range(NCORES))
    if FUSED:
        nc, prog = _build("AB")
        ma, mb = _maps_A(inp), _maps_B(inp)
        maps = [dict(a, **b) for a, b in zip(ma, mb)]
        res = run_bass_kernel_spmd(nc, maps, core_ids=ids)
        return _assemble(res.results)
    ncA, _ = _build("A")
    resA = run_bass_kernel_spmd(ncA, _maps_A(inp), core_ids=ids)
    ncB, _ = _build("B")
    mb = _maps_B(inp)
    for c in range(NCORES):
        for k in ("x1m", "kT", "vsc", "kc", "vc"):
            mb[c][k] = np.asarray(resA.results[c][k])
    resB = run_bass_kernel_spmd(ncB, mb, core_ids=ids)
    return _assemble(resB.results)
```
